# Optimizing a Trainium2 kernel written in Bass

```python
import math
import jax, jax.numpy as jnp
from jax import lax
import numpy as np

D_MODEL = 1024
BATCH = 16
SEQ = 2048
DEPTH = 2

N_A = DEPTH // 2
N_B = DEPTH - N_A
CONV_W = 3
N_HEADS = 16
HEAD_DIM = D_MODEL // N_HEADS
ATT_DIM = N_HEADS * HEAD_DIM
MOBA_BLOCK = 256
MOBA_TOP_K = 3
Q_CHUNK = 128
N_BUCKETS = 32
MAX_EXACT = N_BUCKETS // 2
REL_MAX_DIST = 128
D_FF = 2816
FFN_CONV_W = 3
EPS = 1e-6
NEG = -1e30

kernel_name = "yoco_shortconv_moba_convglu"


def rms_norm(x, g):
    xf = x.astype(jnp.float32)
    y = xf * lax.rsqrt(jnp.mean(xf * xf, axis=-1, keepdims=True) + EPS)
    return (y * g.astype(jnp.float32)).astype(x.dtype)


def causal_dwconv(x, w):
    c = x.shape[-1]
    width = w.shape[0]
    return lax.conv_general_dilated(
        x, w[:, None, :].astype(x.dtype), window_strides=(1,),
        padding=[(width - 1, 0)], dimension_numbers=("NWC", "WIO", "NWC"),
        feature_group_count=c)


def short_conv_mixer(h, w_in, conv_w, w_out):
    b_gate, c_gate, hx = jnp.split(h @ w_in, 3, axis=-1)
    return (b_gate * causal_dwconv(c_gate * hx, conv_w)) @ w_out


def conv_glu_ffn(h, w_up, conv_w, conv_b, w_down):
    u = causal_dwconv(h @ w_up, conv_w) + conv_b
    g, v = jnp.split(u, 2, axis=-1)
    return (jax.nn.silu(g) * v) @ w_down


def rel_bucket(dist):
    n = jnp.maximum(dist, 0)
    is_small = n < MAX_EXACT
    nf = jnp.maximum(n, 1).astype(jnp.float32)
    large = MAX_EXACT + (jnp.log(nf / MAX_EXACT) / math.log(REL_MAX_DIST / MAX_EXACT)
                         * (N_BUCKETS - MAX_EXACT)).astype(jnp.int32)
    large = jnp.minimum(large, N_BUCKETS - 1)
    return jnp.where(is_small, n, large)


def shared_kv(x, kv_norm, w_k, w_v):
    h = rms_norm(x, kv_norm)
    bsz, s, _ = h.shape
    nbp = -(-s // MOBA_BLOCK)
    pad = nbp * MOBA_BLOCK - s
    k = (h @ w_k).reshape(bsz, s, N_HEADS, HEAD_DIM)
    v = (h @ w_v).reshape(bsz, s, N_HEADS, HEAD_DIM)
    k = jnp.pad(k, ((0, 0), (0, pad), (0, 0), (0, 0)))
    v = jnp.pad(v, ((0, 0), (0, pad), (0, 0), (0, 0)))
    k_blk = k.reshape(bsz, nbp, MOBA_BLOCK, N_HEADS, HEAD_DIM).transpose(0, 3, 1, 2, 4)
    v_blk = v.reshape(bsz, nbp, MOBA_BLOCK, N_HEADS, HEAD_DIM).transpose(0, 3, 1, 2, 4)
    count = jnp.clip(s - jnp.arange(nbp) * MOBA_BLOCK, 1, MOBA_BLOCK).astype(jnp.float32)
    k_mean = (k_blk.astype(jnp.float32).sum(axis=3) / count[:, None]).astype(k.dtype)
    return k_blk, v_blk, k_mean


def moba_attention(h, w_q, w_o, k_blk, v_blk, k_mean, rel_bias):
    bsz, s, _ = h.shape
    q = (h @ w_q).reshape(bsz, s, N_HEADS, HEAD_DIM).transpose(0, 2, 1, 3)
    nbp = k_blk.shape[2]
    k_sel = min(MOBA_TOP_K, nbp)
    n_chunks = s // Q_CHUNK
    scale = HEAD_DIM ** -0.5
    rb_t = rel_bias.T
    hi = jnp.arange(N_HEADS)
    own_slot = jnp.arange(k_sel + 1) == k_sel
    in_blk = jnp.arange(MOBA_BLOCK)

    def one_chunk(i):
        b = i // n_chunks
        q0 = (i % n_chunks) * Q_CHUNK
        qc = lax.dynamic_slice(q, (b, 0, q0, 0), (1, N_HEADS, Q_CHUNK, HEAD_DIM))[0]
        kb = lax.dynamic_index_in_dim(k_blk, b, 0, keepdims=False)
        vb = lax.dynamic_index_in_dim(v_blk, b, 0, keepdims=False)
        km = lax.dynamic_index_in_dim(k_mean, b, 0, keepdims=False)
        q_pos = q0 + jnp.arange(Q_CHUNK)
        own = q_pos // MOBA_BLOCK
        gate = jnp.einsum("hqd,hnd->hqn", qc, km).astype(jnp.float32)
        past = jnp.arange(nbp)[None, :] < own[:, None]
        gate = jnp.where(past[None], gate, NEG)
        _, top = lax.top_k(gate, k_sel)
        own_b = jnp.broadcast_to(own[None, :, None], (N_HEADS, Q_CHUNK, 1))
        idx = jnp.concatenate([top, own_b.astype(top.dtype)], axis=-1)
        kg = kb[hi[:, None, None], idx]
        vg = vb[hi[:, None, None], idx]
        sc = jnp.einsum("hqd,hqnkd->hqnk", qc, kg).astype(jnp.float32) * scale
        k_pos = idx[..., None] * MOBA_BLOCK + in_blk
        dist = q_pos[None, :, None, None] - k_pos
        sc = sc + rb_t[hi[:, None, None, None], rel_bucket(dist)].astype(jnp.float32)
        valid = jnp.where(own_slot[None, None, :, None], dist >= 0,
                          (idx < own[None, :, None])[..., None])
        sc = jnp.where(valid, sc, NEG)
        p = jax.nn.softmax(sc.reshape(N_HEADS, Q_CHUNK, -1), axis=-1).reshape(sc.shape)
        return jnp.einsum("hqnk,hqnkd->hqd", p.astype(vg.dtype), vg)

    o = lax.map(one_chunk, jnp.arange(bsz * n_chunks))
    o = o.reshape(bsz, n_chunks, N_HEADS, Q_CHUNK, HEAD_DIM).transpose(0, 1, 3, 2, 4)
    return o.reshape(bsz, s, ATT_DIM) @ w_o


def setup_inputs(seed: int = 0) -> dict:
    key = jax.random.key(seed)
    ks = jax.random.split(key, 20)
    f32 = jnp.float32
    nrm = lambda k, shape, sc: (jax.random.normal(k, shape, f32) * sc)
    gain = lambda k, shape: 1.0 + 0.05 * jax.random.normal(k, shape, f32)
    return {
        "x": jax.random.normal(ks[0], (BATCH, SEQ, D_MODEL), f32),
        "a_norm": gain(ks[1], (N_A, D_MODEL)),
        "a_w_in": nrm(ks[2], (N_A, D_MODEL, 3 * D_MODEL), D_MODEL ** -0.5),
        "a_conv": nrm(ks[3], (N_A, CONV_W, D_MODEL), CONV_W ** -0.5),
        "a_w_out": nrm(ks[4], (N_A, D_MODEL, D_MODEL), D_MODEL ** -0.5),
        "kv_norm": gain(ks[5], (D_MODEL,)),
        "w_k": nrm(ks[6], (D_MODEL, ATT_DIM), D_MODEL ** -0.5),
        "w_v": nrm(ks[7], (D_MODEL, ATT_DIM), D_MODEL ** -0.5),
        "b_norm": gain(ks[8], (N_B, D_MODEL)),
        "b_w_q": nrm(ks[9], (N_B, D_MODEL, ATT_DIM), D_MODEL ** -0.5),
        "b_w_o": nrm(ks[10], (N_B, ATT_DIM, D_MODEL), ATT_DIM ** -0.5),
        "rel_bias": nrm(ks[11], (N_BUCKETS, N_HEADS), 0.5),
        "f_norm": gain(ks[12], (DEPTH, D_MODEL)),
        "f_w_up": nrm(ks[13], (DEPTH, D_MODEL, 2 * D_FF), D_MODEL ** -0.5),
        "f_conv": nrm(ks[14], (DEPTH, FFN_CONV_W, 2 * D_FF), FFN_CONV_W ** -0.5),
        "f_conv_b": nrm(ks[15], (DEPTH, 2 * D_FF), 0.02),
        "f_w_down": nrm(ks[16], (DEPTH, D_FF, D_MODEL), D_FF ** -0.5),
        "final_norm": gain(ks[17], (D_MODEL,)),
    }


def reference(x, a_norm, a_w_in, a_conv, a_w_out, kv_norm, w_k, w_v, b_norm, b_w_q,
              b_w_o, rel_bias, f_norm, f_w_up, f_conv, f_conv_b, f_w_down, final_norm):
    k_blk = v_blk = k_mean = None
    for layer in range(DEPTH):
        if layer < N_A:
            i = layer
            x = x + short_conv_mixer(rms_norm(x, a_norm[i]), a_w_in[i], a_conv[i], a_w_out[i])
        else:
            if layer == N_A:
                k_blk, v_blk, k_mean = shared_kv(x, kv_norm, w_k, w_v)
            j = layer - N_A
            x = x + moba_attention(rms_norm(x, b_norm[j]), b_w_q[j], b_w_o[j],
                                   k_blk, v_blk, k_mean, rel_bias)
        x = x + conv_glu_ffn(rms_norm(x, f_norm[layer]), f_w_up[layer], f_conv[layer],
                             f_conv_b[layer], f_w_down[layer])
    return rms_norm(x, final_norm)
```

```python
import contextlib
import numpy as np
import concourse.bass as bass
import concourse.mybir as mybir
from concourse.bass_utils import run_bass_kernel_spmd

F32 = mybir.dt.float32
BF16 = mybir.dt.bfloat16
AF = mybir.ActivationFunctionType
ALU = mybir.AluOpType
AX = mybir.AxisListType

D = 1024
S = 2048
KC = 8
DFF = 2816
MC = 22
NCORES = 8
EPS = 1e-6
NEGM = -30000.0
PAGE = 256
ARENA_COLS = 104448

G_A, G_F0, G_F1, G_KV, G_B, G_FIN, AC = 0, 8, 16, 24, 32, 40, 48
FC_BASE = 72
NPV = 72 + 2 * 176

COMPUTE = ("pe", "act", "dve", "pool")


class Op:
    __slots__ = ("eng", "kind", "fn", "deps", "idx", "signal", "count", "dsem", "dval")

    def __init__(self, eng, kind, fn):
        self.eng = eng
        self.kind = kind
        self.fn = fn
        self.deps = []
        self.idx = -1
        self.signal = False
        self.count = 0
        self.dsem = None
        self.dval = 0


class Tracker:
    def __init__(self, nc, n_dma_sems=12, serialize=False):
        self.nc = nc
        self.ops = {e: [] for e in ("pe", "act", "dve", "pool", "sp")}
        self.last_w = {}
        self.readers = {}
        self.n_dma_sems = n_dma_sems
        self.dma_ops = {"sp": [], "pool": [], "act": []}
        self.serialize = serialize
        self.prev = None

    def op(self, eng, fn, reads=(), writes=(), kind="c"):
        o = Op(eng, kind, fn)
        lst = self.ops[eng]
        o.idx = len(lst)
        pr = [r for r in reads if r[0] == "P"]
        if pr:
            writes = list(writes) + pr
            reads = [r for r in reads if r[0] != "P"]
        deps = set()
        for r in reads:
            w = self.last_w.get(r)
            if w is not None:
                deps.add(w)
        for r in writes:
            w = self.last_w.get(r)
            if w is not None:
                deps.add(w)
            rl = self.readers.get(r)
            if rl:
                deps.update(rl)
        if self.serialize and self.prev is not None:
            deps.add(self.prev)
        best = {}
        for d in deps:
            if d is o:
                continue
            if d.kind == "c":
                if d.eng == eng and kind == "c" and (eng == "pe" or o.idx - d.idx > 2):
                    continue
                b = best.get(d.eng)
                if b is None or d.idx > b.idx:
                    best[d.eng] = d
            else:
                o.deps.append(d)
        o.deps.extend(best.values())
        for r in reads:
            rl = self.readers.setdefault(r, [])
            if kind == "c":
                rl[:] = [x for x in rl if not (x.kind == "c" and x.eng == eng)]
            rl.append(o)
        for r in writes:
            self.last_w[r] = o
            self.readers[r] = []
        if kind == "d":
            dl = self.dma_ops[eng]
            j = len(dl)
            if j >= self.n_dma_sems:
                o.deps.append(dl[j - self.n_dma_sems])
            dl.append(o)
        lst.append(o)
        self.prev = o
        return o

    def dma(self, eng, out, in_, reads=(), writes=(), **kw):
        return self.op(eng, lambda e: e.dma_start(out=out, in_=in_, **kw), reads, writes, kind="d")

    def finalize_and_emit(self, final_waits=()):
        nc = self.nc
        for e, lst in self.ops.items():
            for o in lst:
                for d in o.deps:
                    if d.kind == "c":
                        d.signal = True
        EPOCH = 1000
        nep = {}
        for e in COMPUTE:
            c = 0
            for o in self.ops[e]:
                if o.kind == "c" and o.signal:
                    o.count = c
                    c += 1
            nep[e] = max(1, (c + EPOCH - 1) // EPOCH)
        with contextlib.ExitStack() as st:
            esem = {e: [st.enter_context(nc.semaphore("S_%s_%d" % (e, i))) for i in range(nep[e])] for e in COMPUTE}
            for q, dl in self.dma_ops.items():
                if dl:
                    sems = [st.enter_context(nc.semaphore("D_%s_%d" % (q, i))) for i in range(self.n_dma_sems)]
                    for j, o in enumerate(dl):
                        o.dsem = sems[j % self.n_dma_sems]
                        o.dval = 16 * (j // self.n_dma_sems + 1)
            block = st.enter_context(nc.Block())

            def target(d):
                if d.kind == "c":
                    return esem[d.eng][d.count // EPOCH], d.count % EPOCH + 1
                return d.dsem, d.dval

            def run(engname, eng, extra=()):
                known = {}
                for o in self.ops[engname]:
                    for d in o.deps:
                        s, v = target(d)
                        k = id(s)
                        if known.get(k, 0) >= v:
                            continue
                        known[k] = v
                        eng.wait_ge(s, v)
                    ins = o.fn(eng)
                    if o.kind == "c":
                        if o.signal:
                            ins.then_inc(esem[engname][o.count // EPOCH], 1)
                    else:
                        ins.then_inc(o.dsem, 16)
                for d in extra:
                    s, v = target(d)
                    eng.wait_ge(s, v)

            @block.tensor
            def _(e):
                run("pe", e)

            @block.scalar
            def _(e):
                run("act", e)

            @block.vector
            def _(e):
                run("dve", e)

            @block.gpsimd
            def _(e):
                run("pool", e)

            @block.sync
            def _(e):
                run("sp", e, extra=final_waits)
        return {e: len(l) for e, l in self.ops.items()}


class Buf:
    def __init__(self, ap, base_col, esz, nfree):
        self.ap = ap
        self.base = base_col
        self.esz = esz
        self.nfree = nfree

    def pg(self, lo=0, hi=None):
        if hi is None:
            hi = self.nfree
        a = (self.base + lo * self.esz) // PAGE
        b = (self.base + hi * self.esz - 1) // PAGE
        return [("A", i) for i in range(a, b + 1)]


class Arena:
    def __init__(self, ap, ncols):
        self.apv = ap
        self.n = ncols
        self.top = 0

    def alloc(self, shape, dtype, parts=128):
        nel = 1
        for s_ in shape:
            nel *= s_
        esz = 2 if dtype == F32 else 1
        lo = (self.top + PAGE - 1) // PAGE * PAGE
        hi = lo + nel * esz
        assert hi <= self.n, "arena overflow %d > %d" % (hi, self.n)
        self.top = hi
        v = self.apv[0:parts, lo:hi]
        if dtype == F32:
            v = v.bitcast(F32)
        if len(shape) == 2:
            v = v.rearrange("p (a b) -> p a b", a=shape[0])
        elif len(shape) == 3:
            v = v.rearrange("p (a b c) -> p a b c", a=shape[0], b=shape[1])
        elif len(shape) == 4:
            v = v.rearrange("p (a b c d) -> p a b c d", a=shape[0], b=shape[1], c=shape[2])
        return Buf(v, lo, esz, nel)


def build_program(nseq=2, phases=("mix", "ffn0", "attn", "ffn1"), final_norm=True, serialize=False, attn_pairs=8, dbg=9):
    nc = bass.Bass("TRN2", target_bir_lowering=False)

    def din(name, shape, dt=F32):
        return nc.dram_tensor(name, list(shape), dt, kind="ExternalInput").ap()

    xT_d = din("xT", [nseq, D, S])
    win_d = din("win", [D, 3 * D])
    wout_d = din("wout", [D, D])
    wo_d = din("wo", [D, D])
    wup_d = din("wup_s", [2, MC, 128, KC * 256])
    wdn_d = din("wdn_s", [2, KC, 128, MC * 128])
    wqkv_d = [din(n, [8, 128, KC * 128]) for n in ("wq_s", "wk_s", "wv_s")]
    pv_d = din("pvec", [128, NPV])
    rb_d = din("rb_aug", [33, 16])
    oht_d = din("oht", [33, 1024])
    id_d = din("ident", [128, 128])
    yT_d = nc.dram_tensor("yT", [nseq, D, S], F32, kind="ExternalOutput").ap()

    def dscr(name, shape, dt=BF16):
        return nc.dram_tensor(name, list(shape), dt, kind="Internal")

    s_win = dscr("s_win", [D, 3 * D]).ap()
    s_wout = dscr("s_wout", [D, D]).ap()
    s_wo = dscr("s_wo", [D, D]).ap()
    s_wup = dscr("s_wup", [2, MC, 128, KC * 256]).ap()
    s_wdn = dscr("s_wdn", [2, KC, 128, MC * 128]).ap()
    gtab_t = dscr("gtab", [16, 1024], F32)
    gs_t = dscr("gs", [16, 128, 1024], BF16)

    with contextlib.ExitStack() as st:
        arena_t = st.enter_context(nc.sbuf_tensor("arena", [128, ARENA_COLS], BF16))
        ps = [st.enter_context(nc.psum_tensor("ps%d" % i, [128, 512], F32)) for i in range(8)]
        PS = lambda i: ("P", i)
        A = Arena(arena_t[:], ARENA_COLS)
        T = Tracker(nc, serialize=serialize)

        xs = A.alloc([KC, S], F32)
        pv = A.alloc([NPV], F32)
        idf = A.alloc([128], F32)
        idb = A.alloc([128], BF16)
        onesb = A.alloc([128], BF16)
        Z = A.alloc([16, 128], BF16)
        halo = A.alloc([2 * MC, 2], F32)
        ksum = A.alloc([2, 8], F32)
        sqb = [A.alloc([512], BF16) for _ in range(2)]
        rsb = [A.alloc([512], F32) for _ in range(2)]
        persist_top = A.top

        def xs_pg(c, t0, t1):
            return xs.pg(c * S + t0, c * S + t1)

        def pvc(col):
            return pv.ap[:, col:col + 1]

        T.dma("sp", pv.ap, pv_d, writes=pv.pg())
        T.dma("sp", idf.ap, id_d, writes=idf.pg())
        T.op("dve", lambda e: e.tensor_copy(out=idb.ap, in_=idf.ap), reads=idf.pg(), writes=idb.pg())
        T.op("dve", lambda e: e.memset(onesb.ap, 1.0), writes=onesb.pg())
        T.op("dve", lambda e: e.memset(Z.ap, 0.0), writes=Z.pg())
        T.op("dve", lambda e: e.tensor_copy(
            out=Z.ap[0:16], in_=idf.ap[0:16, 0:16].unsqueeze(2).to_broadcast([16, 16, 128])),
            reads=idf.pg(), writes=Z.pg())

        def cast(dst, src, nel, key):
            rows = nel // 1024
            r0 = 0
            while r0 < rows:
                r1 = min(rows, r0 + 4096)
                T.dma("pool", dst[r0:r1], src[r0:r1], writes=[key])
                r0 = r1

        def flat(ap):
            nd = len(ap.shape)
            names = " ".join("d%d" % i for i in range(nd))
            return ap.rearrange("%s -> (%s)" % (names, names)).rearrange("(r c) -> r c", c=1024)

        def cast_layer(l):
            for m in range(MC):
                cast(flat(s_wup[l, m]), flat(wup_d[l, m]), 128 * KC * 256, ("W", "up", l, m))
            for c in range(KC):
                cast(flat(s_wdn[l, c]), flat(wdn_d[l, c]), 128 * MC * 128, ("W", "dn", l, c))

        if "mix" in phases:
            cast(flat(s_win), flat(win_d), D * 3 * D, ("W", "win"))
            cast(flat(s_wout), flat(wout_d), D * D, ("W", "wout"))
        if "ffn0" in phases:
            cast_layer(0)

        if "attn" in phases:
            mark = A.top
            rb = A.alloc([16], F32)
            oht = A.alloc([1024], F32)
            gsb = A.alloc([1024], F32)
            T.dma("sp", rb.ap[0:33], rb_d, writes=rb.pg())
            T.dma("sp", oht.ap[0:33], oht_d, writes=oht.pg())
            for hf in range(2):
                T.op("pe", lambda e, hf=hf: e.matmul(ps[hf][0:16, 0:512], lhsT=rb.ap[0:33, :],
                                                      rhs=oht.ap[0:33, hf * 512:(hf + 1) * 512], start=True, stop=True),
                     reads=rb.pg() + oht.pg(), writes=[PS(hf)])
                T.op("dve", lambda e, hf=hf: e.tensor_copy(out=gsb.ap[0:16, hf * 512:(hf + 1) * 512], in_=ps[hf][0:16, 0:512]),
                     reads=[PS(hf)], writes=gsb.pg())
            T.dma("sp", gtab_t.ap(), gsb.ap[0:16], reads=gsb.pg(), writes=[("W", "gtab")])
            for h in range(16):
                T.dma("pool", gs_t.ap()[h], bass.AP(gtab_t, h * 1024, [[0, 128], [1, 1024]]),
                      reads=[("W", "gtab")], writes=[("W", "gs")])
            cast(flat(s_wo), flat(wo_d), D * D, ("W", "wo"))
            A.top = mark
        late_cast_done = [False]

        def rms_stats(t0, W, sq_eng, k):
            rs = rsb[k % 2]
            bank = 6 + (k % 2)
            for c in range(KC):
                sq = sqb[c % 2]
                src = xs.ap[:, c, t0:t0 + W]
                if sq_eng == "act":
                    T.op("act", lambda e, sq=sq, src=src: e.activation(out=sq.ap[:, 0:W], in_=src, func=AF.Square),
                         reads=xs_pg(c, t0, t0 + W), writes=sq.pg())
                else:
                    T.op(sq_eng, lambda e, sq=sq, src=src: e.tensor_tensor(out=sq.ap[:, 0:W], in0=src, in1=src, op=ALU.mult),
                         reads=xs_pg(c, t0, t0 + W), writes=sq.pg())
                T.op("pe", lambda e, sq=sq, c=c: e.matmul(ps[bank][:, 0:W], lhsT=onesb.ap, rhs=sq.ap[:, 0:W],
                                                          start=(c == 0), stop=(c == KC - 1)),
                     reads=sq.pg() + onesb.pg(), writes=[PS(bank)])
            T.op("act", lambda e: e.activation(out=rs.ap[:, 0:W], in_=ps[bank][:, 0:W], func=AF.Sqrt, scale=1.0 / D, bias=EPS),
                 reads=[PS(bank)], writes=rs.pg())
            T.op("dve", lambda e: e.reciprocal(out=rs.ap[:, 0:W], in_=rs.ap[:, 0:W]), reads=rs.pg(), writes=rs.pg())
            return rs

        def norm_to(dst_buf, dst_off, t0, W, gcol, sq_eng, k, nchunkcols):
            rs = rms_stats(t0, W, sq_eng, k)
            for c in range(KC):
                T.op("dve", lambda e, c=c: e.scalar_tensor_tensor(
                    out=dst_buf.ap[:, c, dst_off:dst_off + W], in0=xs.ap[:, c, t0:t0 + W], scalar=pvc(gcol + c),
                    in1=rs.ap[:, 0:W], op0=ALU.mult, op1=ALU.mult),
                    reads=xs_pg(c, t0, t0 + W) + rs.pg() + pv.pg(),
                    writes=dst_buf.pg(c * nchunkcols + dst_off, c * nchunkcols + dst_off + W))

        statk = [0]

        def nk():
            statk[0] += 1
            return statk[0]

        def phase_mixer():
            mark = A.top
            win = A.alloc([KC, 3 * D], BF16)
            wout = A.alloc([KC, D], BF16)
            hb = [A.alloc([KC, 512], BF16) for _ in range(2)]
            yb = A.alloc([KC, 512], BF16)
            cx = A.alloc([KC, 514], F32)
            hxb = [A.alloc([512], F32) for _ in range(2)]
            accb = [A.alloc([512], F32) for _ in range(2)]
            for kc in range(KC):
                T.dma("sp", win.ap[:, kc, :], s_win[kc * 128:(kc + 1) * 128, :], reads=[("W", "win")],
                      writes=win.pg(kc * 3 * D, (kc + 1) * 3 * D))
            for kc in range(KC):
                T.dma("sp", wout.ap[:, kc, :], s_wout[kc * 128:(kc + 1) * 128, :], reads=[("W", "wout")],
                      writes=wout.pg(kc * D, (kc + 1) * D))
            T.op("dve", lambda e: e.memset(cx.ap[:, :, 0:2], 0.0), writes=cx.pg())
            for t in range(S // 512):
                t0 = t * 512
                h = hb[t % 2]
                norm_to(h, 0, t0, 512, G_A, "act", nk(), 512)
                for j in range(KC):
                    banks = (0, 1, 2) if j % 2 == 0 else (3, 4, 5)
                    for gi in range(3):
                        for kc in range(KC):
                            T.op("pe", lambda e, gi=gi, kc=kc, j=j, banks=banks, h=h: e.matmul(
                                ps[banks[gi]][:, :], lhsT=win.ap[:, kc, gi * D + j * 128: gi * D + (j + 1) * 128],
                                rhs=h.ap[:, kc, :], start=(kc == 0), stop=(kc == KC - 1)),
                                reads=win.pg(kc * 3 * D + gi * D + j * 128, kc * 3 * D + gi * D + (j + 1) * 128) + h.pg(kc * 512, (kc + 1) * 512),
                                writes=[PS(banks[gi])])
                    hx = hxb[j % 2]
                    acc = accb[j % 2]
                    cxp = cx.pg(j * 514, (j + 1) * 514)
                    T.op("act", lambda e, hx=hx, banks=banks: e.copy(out=hx.ap, in_=ps[banks[2]][:, :]),
                         reads=[PS(banks[2])], writes=hx.pg())
                    T.op("dve", lambda e, hx=hx, banks=banks, j=j: e.tensor_tensor(
                        out=cx.ap[:, j, 2:514], in0=ps[banks[1]][:, :], in1=hx.ap, op=ALU.mult),
                        reads=[PS(banks[1])] + hx.pg(), writes=cxp)
                    T.op("act", lambda e, acc=acc, j=j: e.activation(out=acc.ap, in_=cx.ap[:, j, 0:512], func=AF.Identity,
                                                                       scale=pvc(AC + 0 * 8 + j)),
                         reads=cxp + pv.pg(), writes=acc.pg())
                    T.op("dve", lambda e, acc=acc, j=j: e.scalar_tensor_tensor(
                        out=acc.ap, in0=cx.ap[:, j, 1:513], scalar=pvc(AC + 1 * 8 + j), in1=acc.ap, op0=ALU.mult, op1=ALU.add),
                        reads=cxp + acc.pg() + pv.pg(), writes=acc.pg())
                    T.op("dve", lambda e, acc=acc, j=j: e.scalar_tensor_tensor(
                        out=acc.ap, in0=cx.ap[:, j, 2:514], scalar=pvc(AC + 2 * 8 + j), in1=acc.ap, op0=ALU.mult, op1=ALU.add),
                        reads=cxp + acc.pg() + pv.pg(), writes=acc.pg())
                    T.op("dve", lambda e, acc=acc, j=j, banks=banks: e.tensor_tensor(
                        out=yb.ap[:, j, :], in0=ps[banks[0]][:, :], in1=acc.ap, op=ALU.mult),
                        reads=[PS(banks[0])] + acc.pg(), writes=yb.pg(j * 512, (j + 1) * 512))
                    T.op("dve", lambda e, j=j: e.tensor_copy(out=cx.ap[:, j, 0:2], in_=cx.ap[:, j, 512:514]),
                         reads=cxp, writes=cxp)
                for c in range(KC):
                    bank = 6 + (c % 2)
                    for kc in range(KC):
                        T.op("pe", lambda e, c=c, kc=kc, bank=bank: e.matmul(
                            ps[bank][:, :], lhsT=wout.ap[:, kc, c * 128:(c + 1) * 128], rhs=yb.ap[:, kc, :],
                            start=(kc == 0), stop=(kc == KC - 1)),
                            reads=wout.pg(kc * D + c * 128, kc * D + (c + 1) * 128) + yb.pg(kc * 512, (kc + 1) * 512),
                            writes=[PS(bank)])
                    T.op("dve", lambda e, c=c, bank=bank, t0=t0: e.tensor_tensor(
                        out=xs.ap[:, c, t0:t0 + 512], in0=ps[bank][:, :], in1=xs.ap[:, c, t0:t0 + 512], op=ALU.add),
                        reads=[PS(bank)] + xs_pg(c, t0, t0 + 512), writes=xs_pg(c, t0, t0 + 512))
            A.top = mark

        def phase_ffn(l):
            mark = A.top
            gcol = G_F0 if l == 0 else G_F1
            fc = FC_BASE + l * 176
            h2 = A.alloc([KC, 1024], BF16)
            act = A.alloc([MC, 1024], BF16)
            wub = [A.alloc([KC, 256], BF16) for _ in range(3)]
            wdb = [A.alloc([MC, 128], BF16) for _ in range(2)]
            upre = [[A.alloc([1026], F32) for _ in range(2)] for _ in range(2)]
            accb = [[A.alloc([1024], F32) for _ in range(2)] for _ in range(2)]
            sgb = [A.alloc([1024], F32) for _ in range(2)]
            T.op("dve", lambda e: e.memset(halo.ap, 0.0), writes=halo.pg())
            for stl in range(S // 1024):
                T0 = stl * 1024
                for tt in range(2):
                    norm_to(h2, tt * 512, T0 + tt * 512, 512, gcol, "pool", nk(), 1024)
                for m in range(MC):
                    wu = wub[m % 3]
                    T.dma("sp", wu.ap.rearrange("p a b -> p (a b)"), s_wup[l, m], reads=[("W", "up", l, m)], writes=wu.pg())
                    banks = (0, 1, 2, 3) if m % 2 == 0 else (4, 5, 6, 7)
                    for tt in range(2):
                        for gv in range(2):
                            for kc in range(KC):
                                T.op("pe", lambda e, tt=tt, gv=gv, kc=kc, wu=wu, banks=banks: e.matmul(
                                    ps[banks[tt * 2 + gv]][:, :], lhsT=wu.ap[:, kc, gv * 128:(gv + 1) * 128],
                                    rhs=h2.ap[:, kc, tt * 512:(tt + 1) * 512], start=(kc == 0), stop=(kc == KC - 1)),
                                    reads=wu.pg(kc * 256 + gv * 128, kc * 256 + (gv + 1) * 128) + h2.pg(kc * 1024 + tt * 512, kc * 1024 + (tt + 1) * 512),
                                    writes=[PS(banks[tt * 2 + gv])])
                    accs = []
                    for gv in range(2):
                        up = upre[gv][m % 2]
                        acc = accb[gv][m % 2]
                        ch = gv * MC + m
                        hp = halo.pg(ch * 2, ch * 2 + 2)
                        T.op("pool", lambda e, up=up, ch=ch: e.tensor_copy(out=up.ap[:, 0:2], in_=halo.ap[:, ch, :]),
                             reads=hp, writes=up.pg(0, 2))
                        for tt in range(2):
                            T.op("act", lambda e, up=up, tt=tt, gv=gv, banks=banks: e.copy(
                                out=up.ap[:, 2 + tt * 512: 2 + (tt + 1) * 512], in_=ps[banks[tt * 2 + gv]][:, :]),
                                reads=[PS(banks[tt * 2 + gv])], writes=up.pg(2 + tt * 512, 2 + (tt + 1) * 512))
                        T.op("pool", lambda e, up=up, ch=ch: e.tensor_copy(out=halo.ap[:, ch, :], in_=up.ap[:, 1024:1026]),
                             reads=up.pg(1024, 1026), writes=hp)
                        T.op("pool", lambda e, up=up, acc=acc, ch=ch: e.tensor_scalar(
                            out=acc.ap, in0=up.ap[:, 0:1024], scalar1=pvc(fc + 0 * 44 + ch), scalar2=pvc(fc + 132 + ch),
                            op0=ALU.mult, op1=ALU.add), reads=up.pg() + pv.pg(), writes=acc.pg())
                        T.op("dve", lambda e, up=up, acc=acc, ch=ch: e.scalar_tensor_tensor(
                            out=acc.ap, in0=up.ap[:, 1:1025], scalar=pvc(fc + 1 * 44 + ch), in1=acc.ap, op0=ALU.mult, op1=ALU.add),
                            reads=up.pg() + acc.pg() + pv.pg(), writes=acc.pg())
                        T.op("dve", lambda e, up=up, acc=acc, ch=ch: e.scalar_tensor_tensor(
                            out=acc.ap, in0=up.ap[:, 2:1026], scalar=pvc(fc + 2 * 44 + ch), in1=acc.ap, op0=ALU.mult, op1=ALU.add),
                            reads=up.pg() + acc.pg() + pv.pg(), writes=acc.pg())
                        accs.append(acc)
                    sg = sgb[m % 2]
                    T.op("act", lambda e, sg=sg, a0=accs[0]: e.activation(out=sg.ap, in_=a0.ap, func=AF.Silu),
                         reads=accs[0].pg(), writes=sg.pg())
                    T.op("dve", lambda e, sg=sg, a1=accs[1], m=m: e.tensor_tensor(out=act.ap[:, m, :], in0=sg.ap, in1=a1.ap, op=ALU.mult),
                         reads=sg.pg() + accs[1].pg(), writes=act.pg(m * 1024, (m + 1) * 1024))
                na = 0
                for c in range(KC):
                    wd = wdb[c % 2]
                    T.dma("sp", wd.ap.rearrange("p a b -> p (a b)"), s_wdn[l, c], reads=[("W", "dn", l, c)], writes=wd.pg())
                    for tt in range(2):
                        bank = na % 8
                        na += 1
                        for m in range(MC):
                            T.op("pe", lambda e, wd=wd, m=m, tt=tt, bank=bank: e.matmul(
                                ps[bank][:, :], lhsT=wd.ap[:, m, :], rhs=act.ap[:, m, tt * 512:(tt + 1) * 512],
                                start=(m == 0), stop=(m == MC - 1)),
                                reads=wd.pg(m * 128, (m + 1) * 128) + act.pg(m * 1024 + tt * 512, m * 1024 + (tt + 1) * 512),
                                writes=[PS(bank)])
                        a0 = T0 + tt * 512
                        T.op("dve", lambda e, c=c, bank=bank, a0=a0: e.tensor_tensor(
                            out=xs.ap[:, c, a0:a0 + 512], in0=ps[bank][:, :], in1=xs.ap[:, c, a0:a0 + 512], op=ALU.add),
                            reads=[PS(bank)] + xs_pg(c, a0, a0 + 512), writes=xs_pg(c, a0, a0 + 512))
            A.top = mark

        def phase_attn():
            mark = A.top
            xn = A.alloc([KC, S], BF16)
            kTb = [A.alloc([S], BF16) for _ in range(2)]
            qTb = [A.alloc([2, S], BF16) for _ in range(2)]
            qTf = A.alloc([1024], F32)
            Vab = [A.alloc([16, 2, 128], BF16) for _ in range(2)]
            wst = [A.alloc([KC, 128], F32) for _ in range(2)]
            wqkv = [A.alloc([KC, 128], BF16) for _ in range(3)]
            negT = A.alloc([1024], BF16)
            Pb = [A.alloc([256], BF16) for _ in range(4)]
            rcb = [A.alloc([256], F32) for _ in range(2)]
            aob = [A.alloc([S], BF16) for _ in range(2)]
            wosl = [A.alloc([D], BF16) for _ in range(2)]
            Tzb = [A.alloc([2, 512], BF16) for _ in range(2)]
            g8 = A.alloc([16, 8], F32)
            mx = A.alloc([16, 8], F32)
            selb = A.alloc([16, 8], F32)
            negm = A.alloc([128], F32)

            for vb in Vab:
                T.op("pool", lambda e, vb=vb: e.memset(vb.ap[:, :, :, 64:128], 1.0), writes=vb.pg())
            for qb in qTb:
                T.op("pool", lambda e, qb=qb: e.memset(qb.ap, 0.0), writes=qb.pg())
            T.op("pool", lambda e: e.memset(negT.ap, 0.0), writes=negT.pg())
            T.op("pool", lambda e: e.memset(ksum.ap, 0.0), writes=ksum.pg())
            for t in range(4):
                rs = rms_stats(t * 512, 512, "pool", nk())
                for c in range(KC):
                    T.op("dve", lambda e, c=c, t=t, rs=rs: e.tensor_tensor(
                        out=xn.ap[:, c, t * 512:(t + 1) * 512], in0=xs.ap[:, c, t * 512:(t + 1) * 512], in1=rs.ap, op=ALU.mult),
                        reads=xs_pg(c, t * 512, (t + 1) * 512) + rs.pg(), writes=xn.pg(c * S + t * 512, c * S + (t + 1) * 512))
            pj = [0]
            cnt = [0, 0]
            nst = [0]
            for p in range(attn_pairs):
                kT = kTb[p % 2]
                qT = qTb[p % 2]
                Va = Vab[p % 2]
                ao = aob[p % 2]
                Tz = Tzb[p % 2]
                if dbg < 1:
                    continue
                for wi, gcol in enumerate((G_B, G_KV, G_KV)):
                    stg = wst[nst[0] % 2]
                    nst[0] += 1
                    T.dma("sp", stg.ap.rearrange("p a b -> p (a b)"), wqkv_d[wi][p], writes=stg.pg())
                    for kc in range(KC):
                        T.op("pool", lambda e, wi=wi, kc=kc, stg=stg, gcol=gcol: e.tensor_scalar(
                            out=wqkv[wi].ap[:, kc, :], in0=stg.ap[:, kc, :], scalar1=pvc(gcol + kc), scalar2=None, op0=ALU.mult),
                            reads=stg.pg(kc * 128, (kc + 1) * 128) + pv.pg(), writes=wqkv[wi].pg(kc * 128, (kc + 1) * 128))
                for h in range(2):
                    hg = 2 * p + h
                    T.dma("sp", Tz.ap[:, h, :], bass.AP(gs_t, hg * 131072 + 127, [[1023, 128], [1, 512]]),
                          reads=[("W", "gs")], writes=Tz.pg(h * 512, (h + 1) * 512))
                T.dma("sp", wosl[p % 2].ap, s_wo[p * 128:(p + 1) * 128, :], reads=[("W", "wo")], writes=wosl[p % 2].pg())
                for t in range(4):
                    bank = pj[0] % 2
                    pj[0] += 1
                    for kc in range(KC):
                        T.op("pe", lambda e, kc=kc, t=t, bank=bank: e.matmul(
                            ps[bank][:, :], lhsT=wqkv[1].ap[:, kc, :], rhs=xn.ap[:, kc, t * 512:(t + 1) * 512],
                            start=(kc == 0), stop=(kc == KC - 1)),
                            reads=wqkv[1].pg(kc * 128, (kc + 1) * 128) + xn.pg(kc * S + t * 512, kc * S + (t + 1) * 512),
                            writes=[PS(bank)])
                    T.op("act", lambda e, t=t, bank=bank, kT=kT: e.copy(out=kT.ap[:, t * 512:(t + 1) * 512], in_=ps[bank][:, :]),
                         reads=[PS(bank)], writes=kT.pg(t * 512, (t + 1) * 512))
                    for h in range(2):
                        T.op("dve", lambda e, t=t, bank=bank, h=h: e.tensor_reduce(
                            out=ksum.ap[h * 64:(h + 1) * 64, h, 2 * t:2 * t + 2],
                            in_=ps[bank][h * 64:(h + 1) * 64, :].rearrange("p (a b) -> p a b", a=2), axis=AX.X, op=ALU.add),
                            reads=[PS(bank)], writes=ksum.pg())
                for t in range(4):
                    bank = pj[0] % 2
                    pj[0] += 1
                    for kc in range(KC):
                        T.op("pe", lambda e, kc=kc, t=t, bank=bank: e.matmul(
                            ps[bank][:, :], lhsT=wqkv[0].ap[:, kc, :], rhs=xn.ap[:, kc, t * 512:(t + 1) * 512],
                            start=(kc == 0), stop=(kc == KC - 1)),
                            reads=wqkv[0].pg(kc * 128, (kc + 1) * 128) + xn.pg(kc * S + t * 512, kc * S + (t + 1) * 512),
                            writes=[PS(bank)])
                    for h in range(2):
                        T.op("act", lambda e, t=t, bank=bank, qT=qT, h=h: e.mul(
                            out=qT.ap[h * 64:(h + 1) * 64, h, t * 512:(t + 1) * 512], in_=ps[bank][h * 64:(h + 1) * 64, :], mul=0.125),
                            reads=[PS(bank)], writes=qT.pg(h * S + t * 512, h * S + (t + 1) * 512))
                    if t >= 2:
                        T.op("dve", lambda e, t=t, bank=bank: e.tensor_scalar(
                            out=qTf.ap[:, (t - 2) * 512:(t - 1) * 512], in0=ps[bank][:, :], scalar1=0.125, scalar2=None, op0=ALU.mult),
                            reads=[PS(bank)], writes=qTf.pg((t - 2) * 512, (t - 1) * 512))
                for k4 in range(4):
                    bank = pj[0] % 2
                    pj[0] += 1
                    for i in range(4):
                        kt = k4 * 4 + i
                        for kc in range(KC):
                            T.op("pe", lambda e, kc=kc, kt=kt, i=i, bank=bank: e.matmul(
                                ps[bank][:, i * 128:(i + 1) * 128], lhsT=xn.ap[:, kc, kt * 128:(kt + 1) * 128], rhs=wqkv[2].ap[:, kc, :],
                                start=(kc == 0), stop=(kc == KC - 1)),
                                reads=wqkv[2].pg(kc * 128, (kc + 1) * 128) + xn.pg(kc * S + kt * 128, kc * S + (kt + 1) * 128),
                                writes=[PS(bank)])
                    T.op("act", lambda e, k4=k4, bank=bank, Va=Va: e.copy(
                        out=Va.ap[:, k4 * 4:(k4 + 1) * 4, :, 0:64], in_=ps[bank][:, :].rearrange("p (i h d) -> p i h d", i=4, h=2)),
                        reads=[PS(bank)], writes=Va.pg(k4 * 4 * 256, (k4 + 1) * 4 * 256))
                if dbg < 2:
                    continue
                for ch in range(8):
                    for h in range(2):
                        T.op("pe", lambda e, ch=ch, h=h: e.matmul(
                            ps[2][:, ch * 16 + h * 8: ch * 16 + h * 8 + 8], lhsT=qTf.ap[:, ch * 128:(ch + 1) * 128],
                            rhs=ksum.ap[:, h, :], start=True, stop=True),
                            reads=qTf.pg(ch * 128, (ch + 1) * 128) + ksum.pg(), writes=[PS(2)])
                T.op("dve", lambda e: e.tensor_copy(out=g8.ap.rearrange("p a b -> p (a b)"), in_=ps[2][:, 0:128]),
                     reads=[PS(2)], writes=g8.pg())
                g8v = g8.ap.rearrange("p (c h) n -> p c h n", h=2)
                for own in range(4, 8):
                    c0 = (own - 4) * 2
                    T.op("dve", lambda e, own=own, c0=c0: e.memset(g8v[:, c0:c0 + 2, :, own:8], -1e30), writes=g8.pg())
                for i in range(16):
                    T.op("dve", lambda e, i=i: e.max(out=mx.ap[:, i, :], in_=g8.ap[:, i, :]), reads=g8.pg(), writes=mx.pg())
                T.op("dve", lambda e: e.tensor_tensor(out=selb.ap, in0=g8.ap, in1=mx.ap[:, :, 2:3].to_broadcast([128, 16, 8]), op=ALU.is_ge),
                     reads=g8.pg() + mx.pg(), writes=selb.pg())
                T.op("dve", lambda e: e.tensor_scalar(out=negm.ap, in0=selb.ap.rearrange("p a b -> p (a b)"),
                                                      scalar1=-NEGM, scalar2=NEGM, op0=ALU.mult, op1=ALU.add),
                     reads=selb.pg(), writes=negm.pg())
                for hf in range(2):
                    for i in range(4):
                        ch = hf * 4 + i
                        T.op("pe", lambda e, ch=ch, i=i: e.transpose(ps[2][0:16, i * 128:(i + 1) * 128], negm.ap[:, ch * 16:(ch + 1) * 16], idf.ap),
                             reads=negm.pg() + idf.pg(), writes=[PS(2)])
                    T.op("dve", lambda e, hf=hf: e.tensor_copy(out=negT.ap[0:16, hf * 512:(hf + 1) * 512], in_=ps[2][0:16, 0:512]),
                         reads=[PS(2)], writes=negT.pg(hf * 512, (hf + 1) * 512))
                if dbg < 3:
                    continue
                for h in range(2):
                    hs = slice(h * 64, (h + 1) * 64)
                    for o in range(8):
                        q0 = o * 256
                        pob = 3 + (cnt[0] % 2)
                        cnt[0] += 1
                        nkt = 2 * (o + 1)
                        for kt in range(nkt):
                            sbk = 5 + (cnt[1] % 3)
                            pt = Pb[cnt[1] % 4]
                            cnt[1] += 1
                            mm = [(kT.ap[:, kt * 128:(kt + 1) * 128], qT.ap[:, h, q0:q0 + 256],
                                   kT.pg(kt * 128, (kt + 1) * 128) + qT.pg(h * S + q0, h * S + q0 + 256))]
                            if kt >= 2 * o - 1:
                                off = (q0 - kt * 128) + 128
                                mm.append((idb.ap, Tz.ap[:, h, off:off + 256], idb.pg() + Tz.pg(h * 512 + off, h * 512 + off + 256)))
                            if o >= 4 and kt < 2 * o:
                                mm.append((Z.ap[:, h * 8 + kt // 2, :], negT.ap[:, q0 - 1024:q0 - 1024 + 256],
                                           Z.pg() + negT.pg(q0 - 1024, q0 - 1024 + 256)))
                            for i, (l_, r_, rd) in enumerate(mm):
                                T.op("pe", lambda e, l_=l_, r_=r_, i=i, n=len(mm), sbk=sbk: e.matmul(
                                    ps[sbk][:, 0:256], lhsT=l_, rhs=r_, start=(i == 0), stop=(i == n - 1)),
                                    reads=rd, writes=[PS(sbk)])
                            T.op("act", lambda e, pt=pt, sbk=sbk: e.activation(out=pt.ap, in_=ps[sbk][:, 0:256], func=AF.Exp),
                                 reads=[PS(sbk)], writes=pt.pg())
                            T.op("pe", lambda e, pt=pt, kt=kt, h=h, pob=pob, nkt=nkt, Va=Va: e.matmul(
                                ps[pob][:, 0:256], lhsT=Va.ap[:, kt, h, :], rhs=pt.ap, start=(kt == 0), stop=(kt == nkt - 1)),
                                reads=pt.pg() + Va.pg(kt * 256 + h * 128, kt * 256 + (h + 1) * 128), writes=[PS(pob)])
                        rc = rcb[cnt[0] % 2]
                        T.op("dve", lambda e, rc=rc, pob=pob: e.reciprocal(out=rc.ap[64:128, :], in_=ps[pob][64:128, 0:256]),
                             reads=[PS(pob)], writes=rc.pg())
                        T.op("dve", lambda e, rc=rc, pob=pob, hs=hs, q0=q0, ao=ao: e.tensor_tensor(
                            out=ao.ap[hs, q0:q0 + 256], in0=ps[pob][0:64, 0:256], in1=rc.ap[64:128, :], op=ALU.mult),
                            reads=[PS(pob)] + rc.pg(), writes=ao.pg(q0, q0 + 256))
                if dbg < 4:
                    continue
                for t in range(4):
                    for c in range(KC):
                        bank = pj[0] % 2
                        pj[0] += 1
                        T.op("pe", lambda e, c=c, t=t, bank=bank, ao=ao, w=wosl[p % 2]: e.matmul(
                            ps[bank][:, :], lhsT=w.ap[:, c * 128:(c + 1) * 128], rhs=ao.ap[:, t * 512:(t + 1) * 512], start=True, stop=True),
                            reads=wosl[p % 2].pg(c * 128, (c + 1) * 128) + ao.pg(t * 512, (t + 1) * 512), writes=[PS(bank)])
                        T.op("dve", lambda e, c=c, t=t, bank=bank: e.tensor_tensor(
                            out=xs.ap[:, c, t * 512:(t + 1) * 512], in0=ps[bank][:, :], in1=xs.ap[:, c, t * 512:(t + 1) * 512], op=ALU.add),
                            reads=[PS(bank)] + xs_pg(c, t * 512, (t + 1) * 512), writes=xs_pg(c, t * 512, (t + 1) * 512))
            A.top = mark

        def phase_out(s, do_norm):
            mark = A.top
            ob = [A.alloc([512], F32) for _ in range(3)]
            k = 0
            outs = []
            for t in range(S // 512):
                t0 = t * 512
                if do_norm:
                    rs = rms_stats(t0, 512, "pool", nk())
                for c in range(KC):
                    o_ = ob[k % 3]
                    k += 1
                    if do_norm:
                        T.op("dve", lambda e, c=c, t0=t0, o_=o_, rs=rs: e.scalar_tensor_tensor(
                            out=o_.ap, in0=xs.ap[:, c, t0:t0 + 512], scalar=pvc(G_FIN + c), in1=rs.ap, op0=ALU.mult, op1=ALU.mult),
                            reads=xs_pg(c, t0, t0 + 512) + rs.pg() + pv.pg(), writes=o_.pg())
                    else:
                        T.op("dve", lambda e, c=c, t0=t0, o_=o_: e.tensor_copy(out=o_.ap, in_=xs.ap[:, c, t0:t0 + 512]),
                             reads=xs_pg(c, t0, t0 + 512), writes=o_.pg())
                    outs.append(T.dma("sp", yT_d[s, c * 128:(c + 1) * 128, t0:t0 + 512], o_.ap, reads=o_.pg()))
            A.top = mark
            return outs

        final = []
        for s in range(nseq):
            for c in range(KC):
                T.dma("sp", xs.ap[:, c, :], xT_d[s, c * 128:(c + 1) * 128, :], writes=xs_pg(c, 0, S))
            if "mix" in phases:
                phase_mixer()
            if "ffn0" in phases:
                phase_ffn(0)
            if not late_cast_done[0]:
                if "ffn1" in phases:
                    cast_layer(1)
                late_cast_done[0] = True
            if "attn" in phases:
                phase_attn()
            if "ffn1" in phases:
                phase_ffn(1)
            final += phase_out(s, final_norm)
        counts = T.finalize_and_emit(final_waits=final)
    return nc, counts


def _rel_bucket(n):
    import math
    n = max(n, 0)
    if n < 16:
        return n
    v = 16 + int(np.float32(np.log(np.float32(n) / np.float32(16)) / np.float32(math.log(128 / 16))) * np.float32(16))
    return min(v, 31)


def _const_tables():
    oht = np.zeros((33, 1024), np.float32)
    for m in range(1024):
        d = m - 255
        if d < 0:
            oht[32, m] = NEGM
        else:
            b = _rel_bucket(d)
            oht[b, m] += 1.0
            oht[31, m] -= 1.0
    return oht, np.eye(128, dtype=np.float32)


def prep_shared(a_norm, a_w_in, a_conv, a_w_out, kv_norm, w_k, w_v, b_norm, b_w_q, b_w_o, rel_bias,
                f_norm, f_w_up, f_conv, f_conv_b, f_w_down, final_norm):
    f = np.float32
    pvm = np.zeros((128, NPV), f)

    def put(col, vec):
        n = vec.shape[0] // 128
        pvm[:, col:col + n] = np.asarray(vec, f).reshape(n, 128).T

    put(G_A, a_norm[0]); put(G_F0, f_norm[0]); put(G_F1, f_norm[1]); put(G_KV, kv_norm)
    put(G_B, b_norm[0]); put(G_FIN, final_norm)
    for j in range(3):
        put(AC + j * 8, a_conv[0, j])
    for l in range(2):
        base = FC_BASE + l * 176
        for j in range(3):
            put(base + j * 44, f_conv[l, j])
        put(base + 132, f_conv_b[l])
    wup_s = np.empty((2, MC, 128, KC * 256), f)
    wdn_s = np.empty((2, KC, 128, MC * 128), f)
    for l in range(2):
        w = np.asarray(f_w_up[l], f).reshape(KC, 128, 2, MC, 128)
        wup_s[l] = w.transpose(3, 1, 0, 2, 4).reshape(MC, 128, KC * 256)
        w = np.asarray(f_w_down[l], f).reshape(MC, 128, KC, 128)
        wdn_s[l] = w.transpose(2, 1, 0, 3).reshape(KC, 128, MC * 128)

    def pair_slabs(w):
        w = np.asarray(w, f).reshape(KC, 128, 8, 128)
        return np.ascontiguousarray(w.transpose(2, 1, 0, 3).reshape(8, 128, KC * 128))

    oht, ident = _const_tables()
    rb_aug = np.concatenate([np.asarray(rel_bias, f), np.ones((1, 16), f)], axis=0)
    return {
        "win": np.ascontiguousarray(a_w_in[0], f), "wout": np.ascontiguousarray(a_w_out[0], f),
        "wo": np.ascontiguousarray(b_w_o[0], f), "wup_s": wup_s, "wdn_s": wdn_s,
        "wq_s": pair_slabs(b_w_q[0]), "wk_s": pair_slabs(w_k), "wv_s": pair_slabs(w_v),
        "pvec": pvm, "rb_aug": rb_aug, "oht": oht, "ident": ident,
    }


_PROG = {}


def kernel(x, a_norm, a_w_in, a_conv, a_w_out, kv_norm, w_k, w_v, b_norm, b_w_q, b_w_o, rel_bias,
           f_norm, f_w_up, f_conv, f_conv_b, f_w_down, final_norm):
    x = np.asarray(x, np.float32)
    shared = prep_shared(a_norm, a_w_in, a_conv, a_w_out, kv_norm, w_k, w_v, b_norm, b_w_q, b_w_o, rel_bias,
                         f_norm, f_w_up, f_conv, f_conv_b, f_w_down, final_norm)
    if "nc" not in _PROG:
        _PROG["nc"] = build_program(nseq=1)[0]
    nc = _PROG["nc"]
    out = np.empty((16, S, D), np.float32)
    for half in range(2):
        in_maps = []
        for c in range(NCORES):
            b = 2 * c + half
            m = dict(shared)
            m["xT"] = np.ascontiguousarray(x[b:b + 1].transpose(0, 2, 1))
            in_maps.append(m)
        res = run_bass_kernel_spmd(nc, in_maps, core_ids=list(range(NCORES)))
        for c in range(NCORES):
            out[2 * c + half] = np.asarray(res.results[c]["yT"])[0].T
    return out
```

```python
import contextlib
import numpy as np
import concourse.bass as bass
import concourse.mybir as mybir
from concourse.bass_utils import run_bass_kernel_spmd

F32 = mybir.dt.float32
BF16 = mybir.dt.bfloat16
AF = mybir.ActivationFunctionType
ALU = mybir.AluOpType
AX = mybir.AxisListType

D = 1024
S = 2048
KC = 8
DFF = 2816
MC = 22
NCORES = 8
EPS = 1e-6
NEGM = -30000.0
PAGE = 256
ARENA_COLS = 104448

G_A, G_F0, G_F1, G_KV, G_B, G_FIN, AC = 0, 8, 16, 24, 32, 40, 48
FC_BASE = 72
NPV = 72 + 2 * 176

COMPUTE = ("pe", "act", "dve", "pool")


class Op:
    __slots__ = ("eng", "kind", "fn", "deps", "idx", "signal", "count", "dsem", "dval")

    def __init__(self, eng, kind, fn):
        self.eng = eng
        self.kind = kind
        self.fn = fn
        self.deps = []
        self.idx = -1
        self.signal = False
        self.count = 0
        self.dsem = None
        self.dval = 0


class Tracker:
    def __init__(self, nc, n_dma_sems=8, serialize=False):
        self.nc = nc
        self.ops = {e: [] for e in ("pe", "act", "dve", "pool", "sp")}
        self.last_w = {}
        self.readers = {}
        self.n_dma_sems = n_dma_sems
        self.dma_ops = {"sp": [], "pool": [], "act": []}
        self.serialize = serialize
        self.prev = None
        self.marks = []

    def mark(self):
        self.marks.append({e: len(l) for e, l in self.ops.items()})

    def op(self, eng, fn, reads=(), writes=(), kind="c"):
        o = Op(eng, kind, fn)
        lst = self.ops[eng]
        o.idx = len(lst)
        pr = [r for r in reads if r[0] == "P"]
        if pr:
            writes = list(writes) + pr
            reads = [r for r in reads if r[0] != "P"]
        deps = set()
        for r in reads:
            w = self.last_w.get(r)
            if w is not None:
                deps.add(w)
        for r in writes:
            w = self.last_w.get(r)
            if w is not None:
                deps.add(w)
            rl = self.readers.get(r)
            if rl:
                deps.update(rl)
        if self.serialize and self.prev is not None:
            deps.add(self.prev)
        best = {}
        for d in deps:
            if d is o:
                continue
            if d.kind == "c":
                if d.eng == eng and kind == "c" and (eng == "pe" or o.idx - d.idx > 2):
                    continue
                b = best.get(d.eng)
                if b is None or d.idx > b.idx:
                    best[d.eng] = d
            else:
                o.deps.append(d)
        o.deps.extend(best.values())
        for r in reads:
            rl = self.readers.setdefault(r, [])
            if kind == "c":
                rl[:] = [x for x in rl if not (x.kind == "c" and x.eng == eng)]
            rl.append(o)
        for r in writes:
            self.last_w[r] = o
            self.readers[r] = []
        if kind == "d":
            dl = self.dma_ops[eng]
            j = len(dl)
            if j >= self.n_dma_sems:
                o.deps.append(dl[j - self.n_dma_sems])
            dl.append(o)
        lst.append(o)
        self.prev = o
        return o

    def dma(self, eng, out, in_, reads=(), writes=(), **kw):
        return self.op(eng, lambda e: e.dma_start(out=out, in_=in_, **kw), reads, writes, kind="d")

    def finalize_and_emit(self, final_waits=()):
        nc = self.nc
        for e, lst in self.ops.items():
            for o in lst:
                for d in o.deps:
                    if d.kind == "c":
                        d.signal = True
        EPOCH = 1800
        nep = {}
        for e in COMPUTE:
            c = 0
            for o in self.ops[e]:
                if o.kind == "c" and o.signal:
                    o.count = c
                    c += 1
            nep[e] = max(1, (c + EPOCH - 1) // EPOCH)
        with contextlib.ExitStack() as st:
            esem = {e: [st.enter_context(nc.semaphore("S_%s_%d" % (e, i))) for i in range(nep[e])] for e in COMPUTE}
            for q, dl in self.dma_ops.items():
                if dl:
                    sems = [st.enter_context(nc.semaphore("D_%s_%d" % (q, i))) for i in range(self.n_dma_sems)]
                    for j, o in enumerate(dl):
                        o.dsem = sems[j % self.n_dma_sems]
                        o.dval = 16 * (j // self.n_dma_sems + 1)
            def target(d):
                if d.kind == "c":
                    return esem[d.eng][d.count // EPOCH], d.count % EPOCH + 1
                return d.dsem, d.dval

            known_all = {e: {} for e in self.ops}

            def run(engname, eng, lo, hi, extra=()):
                known = known_all[engname]
                for o in self.ops[engname][lo:hi]:
                    for d in o.deps:
                        s, v = target(d)
                        k = id(s)
                        if known.get(k, 0) >= v:
                            continue
                        known[k] = v
                        eng.wait_ge(s, v)
                    ins = o.fn(eng)
                    if o.kind == "c":
                        if o.signal:
                            ins.then_inc(esem[engname][o.count // EPOCH], 1)
                    else:
                        ins.then_inc(o.dsem, 16)
                for d in extra:
                    s, v = target(d)
                    eng.wait_ge(s, v)

            bounds = [dict((e, 0) for e in self.ops)] + self.marks + [{e: len(l) for e, l in self.ops.items()}]
            nseg = len(bounds) - 1
            for si in range(nseg):
                lo, hi = bounds[si], bounds[si + 1]
                last = (si == nseg - 1)
                with nc.Block() as block:
                    @block.tensor
                    def _(e):
                        run("pe", e, lo["pe"], hi["pe"])

                    @block.scalar
                    def _(e):
                        run("act", e, lo["act"], hi["act"])

                    @block.vector
                    def _(e):
                        run("dve", e, lo["dve"], hi["dve"])

                    @block.gpsimd
                    def _(e):
                        run("pool", e, lo["pool"], hi["pool"])

                    @block.sync
                    def _(e):
                        run("sp", e, lo["sp"], hi["sp"], extra=final_waits if last else ())
        return {e: len(l) for e, l in self.ops.items()}


class Buf:
    def __init__(self, ap, base_col, esz, nfree):
        self.ap = ap
        self.base = base_col
        self.esz = esz
        self.nfree = nfree

    def pg(self, lo=0, hi=None):
        if hi is None:
            hi = self.nfree
        a = (self.base + lo * self.esz) // PAGE
        b = (self.base + hi * self.esz - 1) // PAGE
        return [("A", i) for i in range(a, b + 1)]


class Arena:
    def __init__(self, ap, ncols):
        self.apv = ap
        self.n = ncols
        self.top = 0

    def alloc(self, shape, dtype, parts=128):
        nel = 1
        for s_ in shape:
            nel *= s_
        esz = 2 if dtype == F32 else 1
        lo = (self.top + PAGE - 1) // PAGE * PAGE
        hi = lo + nel * esz
        assert hi <= self.n, "arena overflow %d > %d" % (hi, self.n)
        self.top = hi
        v = self.apv[0:parts, lo:hi]
        if dtype == F32:
            v = v.bitcast(F32)
        if len(shape) == 2:
            v = v.rearrange("p (a b) -> p a b", a=shape[0])
        elif len(shape) == 3:
            v = v.rearrange("p (a b c) -> p a b c", a=shape[0], b=shape[1])
        elif len(shape) == 4:
            v = v.rearrange("p (a b c d) -> p a b c d", a=shape[0], b=shape[1], c=shape[2])
        return Buf(v, lo, esz, nel)


def build_program(nseq=2, phases=("mix", "ffn0", "attn", "ffn1"), final_norm=True, serialize=False, attn_pairs=8, dbg=9):
    nc = bass.Bass("TRN2", target_bir_lowering=False)

    def din(name, shape, dt=F32):
        return nc.dram_tensor(name, list(shape), dt, kind="ExternalInput").ap()

    xT_d = din("xT", [nseq, D, S])
    win_d = din("win", [D, 3 * D])
    wout_d = din("wout", [D, D])
    wo_d = din("wo", [D, D])
    wup_d = din("wup_s", [2, MC, 128, KC * 256])
    wdn_d = din("wdn_s", [2, KC, 128, MC * 128])
    wqkv_d = [din(n, [8, 128, KC * 128]) for n in ("wq_s", "wk_s", "wv_s")]
    pv_d = din("pvec", [128, NPV])
    rb_d = din("rb_aug", [33, 16])
    oht_d = din("oht", [33, 1024])
    id_d = din("ident", [128, 128])
    yT_d = nc.dram_tensor("yT", [nseq, D, S], F32, kind="ExternalOutput").ap()

    def dscr(name, shape, dt=BF16):
        return nc.dram_tensor(name, list(shape), dt, kind="Internal")

    s_win = dscr("s_win", [D, 3 * D]).ap()
    s_wout = dscr("s_wout", [D, D]).ap()
    s_wo = dscr("s_wo", [D, D]).ap()
    s_wup = dscr("s_wup", [2, MC, 128, KC * 256]).ap()
    s_wdn = dscr("s_wdn", [2, KC, 128, MC * 128]).ap()
    gtab_t = dscr("gtab", [16, 1024], F32)
    gs_t = dscr("gs", [16, 128, 1024], BF16)

    with contextlib.ExitStack() as st:
        arena_t = st.enter_context(nc.sbuf_tensor("arena", [128, ARENA_COLS], BF16))
        ps = [st.enter_context(nc.psum_tensor("ps%d" % i, [128, 512], F32)) for i in range(8)]
        PS = lambda i: ("P", i)
        A = Arena(arena_t[:], ARENA_COLS)
        T = Tracker(nc, serialize=serialize)

        xs = A.alloc([KC, S], F32)
        pv = A.alloc([NPV], F32)
        idf = A.alloc([128], F32)
        idb = A.alloc([128], BF16)
        onesb = A.alloc([128], BF16)
        Z = A.alloc([16, 128], BF16)
        halo = A.alloc([2 * MC, 2], F32)
        ksum = A.alloc([2, 8], F32)
        sqb = [A.alloc([512], BF16) for _ in range(2)]
        rsb = [A.alloc([512], F32) for _ in range(2)]
        persist_top = A.top

        def xs_pg(c, t0, t1):
            return xs.pg(c * S + t0, c * S + t1)

        def pvc(col):
            return pv.ap[:, col:col + 1]

        T.dma("sp", pv.ap, pv_d, writes=pv.pg())
        T.dma("sp", idf.ap, id_d, writes=idf.pg())
        T.op("dve", lambda e: e.tensor_copy(out=idb.ap, in_=idf.ap), reads=idf.pg(), writes=idb.pg())
        T.op("dve", lambda e: e.memset(onesb.ap, 1.0), writes=onesb.pg())
        T.op("dve", lambda e: e.memset(Z.ap, 0.0), writes=Z.pg())
        T.op("dve", lambda e: e.tensor_copy(
            out=Z.ap[0:16], in_=idf.ap[0:16, 0:16].unsqueeze(2).to_broadcast([16, 16, 128])),
            reads=idf.pg(), writes=Z.pg())

        def cast(dst, src, nel, key):
            rows = nel // 1024
            r0 = 0
            while r0 < rows:
                r1 = min(rows, r0 + 4096)
                T.dma("pool", dst[r0:r1], src[r0:r1], writes=[key])
                r0 = r1

        def flat(ap):
            nd = len(ap.shape)
            names = " ".join("d%d" % i for i in range(nd))
            return ap.rearrange("%s -> (%s)" % (names, names)).rearrange("(r c) -> r c", c=1024)

        def cast_layer(l):
            for m in range(MC):
                cast(flat(s_wup[l, m]), flat(wup_d[l, m]), 128 * KC * 256, ("W", "up", l, m))
            for c in range(KC):
                cast(flat(s_wdn[l, c]), flat(wdn_d[l, c]), 128 * MC * 128, ("W", "dn", l, c))

        if "mix" in phases:
            cast(flat(s_win), flat(win_d), D * 3 * D, ("W", "win"))
            cast(flat(s_wout), flat(wout_d), D * D, ("W", "wout"))
        if "ffn0" in phases:
            cast_layer(0)

        if "attn" in phases:
            mark = A.top
            rb = A.alloc([16], F32)
            oht = A.alloc([1024], F32)
            gsb = A.alloc([1024], F32)
            T.dma("sp", rb.ap[0:33], rb_d, writes=rb.pg())
            T.dma("sp", oht.ap[0:33], oht_d, writes=oht.pg())
            for hf in range(2):
                T.op("pe", lambda e, hf=hf: e.matmul(ps[hf][0:16, 0:512], lhsT=rb.ap[0:33, :],
                                                      rhs=oht.ap[0:33, hf * 512:(hf + 1) * 512], start=True, stop=True),
                     reads=rb.pg() + oht.pg(), writes=[PS(hf)])
                T.op("dve", lambda e, hf=hf: e.tensor_copy(out=gsb.ap[0:16, hf * 512:(hf + 1) * 512], in_=ps[hf][0:16, 0:512]),
                     reads=[PS(hf)], writes=gsb.pg())
            T.dma("sp", gtab_t.ap(), gsb.ap[0:16], reads=gsb.pg(), writes=[("W", "gtab")])
            for h in range(16):
                T.dma("pool", gs_t.ap()[h], bass.AP(gtab_t, h * 1024, [[0, 128], [1, 1024]]),
                      reads=[("W", "gtab")], writes=[("W", "gs")])
            cast(flat(s_wo), flat(wo_d), D * D, ("W", "wo"))
            A.top = mark
        late_cast_done = [False]

        def rms_stats(t0, W, sq_eng, k):
            rs = rsb[k % 2]
            bank = 6 + (k % 2)
            for c in range(KC):
                sq = sqb[c % 2]
                src = xs.ap[:, c, t0:t0 + W]
                if sq_eng == "act":
                    T.op("act", lambda e, sq=sq, src=src: e.activation(out=sq.ap[:, 0:W], in_=src, func=AF.Square),
                         reads=xs_pg(c, t0, t0 + W), writes=sq.pg())
                else:
                    T.op(sq_eng, lambda e, sq=sq, src=src: e.tensor_tensor(out=sq.ap[:, 0:W], in0=src, in1=src, op=ALU.mult),
                         reads=xs_pg(c, t0, t0 + W), writes=sq.pg())
                T.op("pe", lambda e, sq=sq, c=c: e.matmul(ps[bank][:, 0:W], lhsT=onesb.ap, rhs=sq.ap[:, 0:W],
                                                          start=(c == 0), stop=(c == KC - 1)),
                     reads=sq.pg() + onesb.pg(), writes=[PS(bank)])
            T.op("act", lambda e: e.activation(out=rs.ap[:, 0:W], in_=ps[bank][:, 0:W], func=AF.Sqrt, scale=1.0 / D, bias=EPS),
                 reads=[PS(bank)], writes=rs.pg())
            T.op("dve", lambda e: e.reciprocal(out=rs.ap[:, 0:W], in_=rs.ap[:, 0:W]), reads=rs.pg(), writes=rs.pg())
            return rs

        def norm_to(dst_buf, dst_off, t0, W, gcol, sq_eng, k, nchunkcols):
            rs = rms_stats(t0, W, sq_eng, k)
            for c in range(KC):
                T.op("dve", lambda e, c=c: e.scalar_tensor_tensor(
                    out=dst_buf.ap[:, c, dst_off:dst_off + W], in0=xs.ap[:, c, t0:t0 + W], scalar=pvc(gcol + c),
                    in1=rs.ap[:, 0:W], op0=ALU.mult, op1=ALU.mult),
                    reads=xs_pg(c, t0, t0 + W) + rs.pg() + pv.pg(),
                    writes=dst_buf.pg(c * nchunkcols + dst_off, c * nchunkcols + dst_off + W))

        statk = [0]

        def nk():
            statk[0] += 1
            return statk[0]

        def phase_mixer():
            mark = A.top
            win = A.alloc([KC, 3 * D], BF16)
            wout = A.alloc([KC, D], BF16)
            hb = [A.alloc([KC, 512], BF16) for _ in range(2)]
            yb = A.alloc([KC, 512], BF16)
            cx = A.alloc([KC, 514], F32)
            hxb = [A.alloc([512], F32) for _ in range(2)]
            accb = [A.alloc([512], F32) for _ in range(2)]
            for kc in range(KC):
                T.dma("sp", win.ap[:, kc, :], s_win[kc * 128:(kc + 1) * 128, :], reads=[("W", "win")],
                      writes=win.pg(kc * 3 * D, (kc + 1) * 3 * D))
            for kc in range(KC):
                T.dma("sp", wout.ap[:, kc, :], s_wout[kc * 128:(kc + 1) * 128, :], reads=[("W", "wout")],
                      writes=wout.pg(kc * D, (kc + 1) * D))
            T.op("dve", lambda e: e.memset(cx.ap[:, :, 0:2], 0.0), writes=cx.pg())
            for t in range(S // 512):
                t0 = t * 512
                h = hb[t % 2]
                norm_to(h, 0, t0, 512, G_A, "act", nk(), 512)
                for j in range(KC):
                    banks = (0, 1, 2) if j % 2 == 0 else (3, 4, 5)
                    for gi in range(3):
                        for kc in range(KC):
                            T.op("pe", lambda e, gi=gi, kc=kc, j=j, banks=banks, h=h: e.matmul(
                                ps[banks[gi]][:, :], lhsT=win.ap[:, kc, gi * D + j * 128: gi * D + (j + 1) * 128],
                                rhs=h.ap[:, kc, :], start=(kc == 0), stop=(kc == KC - 1)),
                                reads=win.pg(kc * 3 * D + gi * D + j * 128, kc * 3 * D + gi * D + (j + 1) * 128) + h.pg(kc * 512, (kc + 1) * 512),
                                writes=[PS(banks[gi])])
                    hx = hxb[j % 2]
                    acc = accb[j % 2]
                    cxp = cx.pg(j * 514, (j + 1) * 514)
                    T.op("act", lambda e, hx=hx, banks=banks: e.copy(out=hx.ap, in_=ps[banks[2]][:, :]),
                         reads=[PS(banks[2])], writes=hx.pg())
                    T.op("dve", lambda e, hx=hx, banks=banks, j=j: e.tensor_tensor(
                        out=cx.ap[:, j, 2:514], in0=ps[banks[1]][:, :], in1=hx.ap, op=ALU.mult),
                        reads=[PS(banks[1])] + hx.pg(), writes=cxp)
                    T.op("act", lambda e, acc=acc, j=j: e.activation(out=acc.ap, in_=cx.ap[:, j, 0:512], func=AF.Identity,
                                                                       scale=pvc(AC + 0 * 8 + j)),
                         reads=cxp + pv.pg(), writes=acc.pg())
                    T.op("dve", lambda e, acc=acc, j=j: e.scalar_tensor_tensor(
                        out=acc.ap, in0=cx.ap[:, j, 1:513], scalar=pvc(AC + 1 * 8 + j), in1=acc.ap, op0=ALU.mult, op1=ALU.add),
                        reads=cxp + acc.pg() + pv.pg(), writes=acc.pg())
                    T.op("dve", lambda e, acc=acc, j=j: e.scalar_tensor_tensor(
                        out=acc.ap, in0=cx.ap[:, j, 2:514], scalar=pvc(AC + 2 * 8 + j), in1=acc.ap, op0=ALU.mult, op1=ALU.add),
                        reads=cxp + acc.pg() + pv.pg(), writes=acc.pg())
                    T.op("dve", lambda e, acc=acc, j=j, banks=banks: e.tensor_tensor(
                        out=yb.ap[:, j, :], in0=ps[banks[0]][:, :], in1=acc.ap, op=ALU.mult),
                        reads=[PS(banks[0])] + acc.pg(), writes=yb.pg(j * 512, (j + 1) * 512))
                    T.op("dve", lambda e, j=j: e.tensor_copy(out=cx.ap[:, j, 0:2], in_=cx.ap[:, j, 512:514]),
                         reads=cxp, writes=cxp)
                for c in range(KC):
                    bank = 6 + (c % 2)
                    for kc in range(KC):
                        T.op("pe", lambda e, c=c, kc=kc, bank=bank: e.matmul(
                            ps[bank][:, :], lhsT=wout.ap[:, kc, c * 128:(c + 1) * 128], rhs=yb.ap[:, kc, :],
                            start=(kc == 0), stop=(kc == KC - 1)),
                            reads=wout.pg(kc * D + c * 128, kc * D + (c + 1) * 128) + yb.pg(kc * 512, (kc + 1) * 512),
                            writes=[PS(bank)])
                    T.op("dve", lambda e, c=c, bank=bank, t0=t0: e.tensor_tensor(
                        out=xs.ap[:, c, t0:t0 + 512], in0=ps[bank][:, :], in1=xs.ap[:, c, t0:t0 + 512], op=ALU.add),
                        reads=[PS(bank)] + xs_pg(c, t0, t0 + 512), writes=xs_pg(c, t0, t0 + 512))
            A.top = mark

        def phase_ffn(l):
            mark = A.top
            gcol = G_F0 if l == 0 else G_F1
            fc = FC_BASE + l * 176
            h2 = A.alloc([KC, 1024], BF16)
            act = A.alloc([MC, 1024], BF16)
            wub = [A.alloc([KC, 256], BF16) for _ in range(3)]
            wdb = [A.alloc([MC, 128], BF16) for _ in range(2)]
            upre = [[A.alloc([1026], F32) for _ in range(2)] for _ in range(2)]
            accb = [[A.alloc([1024], F32) for _ in range(2)] for _ in range(2)]
            sgb = [A.alloc([1024], F32) for _ in range(2)]
            T.op("dve", lambda e: e.memset(halo.ap, 0.0), writes=halo.pg())
            for stl in range(S // 1024):
                T0 = stl * 1024
                for tt in range(2):
                    norm_to(h2, tt * 512, T0 + tt * 512, 512, gcol, "pool", nk(), 1024)
                for m in range(MC):
                    wu = wub[m % 3]
                    T.dma("sp", wu.ap.rearrange("p a b -> p (a b)"), s_wup[l, m], reads=[("W", "up", l, m)], writes=wu.pg())
                    banks = (0, 1, 2, 3) if m % 2 == 0 else (4, 5, 6, 7)
                    for tt in range(2):
                        for gv in range(2):
                            for kc in range(KC):
                                T.op("pe", lambda e, tt=tt, gv=gv, kc=kc, wu=wu, banks=banks: e.matmul(
                                    ps[banks[tt * 2 + gv]][:, :], lhsT=wu.ap[:, kc, gv * 128:(gv + 1) * 128],
                                    rhs=h2.ap[:, kc, tt * 512:(tt + 1) * 512], start=(kc == 0), stop=(kc == KC - 1)),
                                    reads=wu.pg(kc * 256 + gv * 128, kc * 256 + (gv + 1) * 128) + h2.pg(kc * 1024 + tt * 512, kc * 1024 + (tt + 1) * 512),
                                    writes=[PS(banks[tt * 2 + gv])])
                    accs = []
                    for gv in range(2):
                        up = upre[gv][m % 2]
                        acc = accb[gv][m % 2]
                        ch = gv * MC + m
                        hp = halo.pg(ch * 2, ch * 2 + 2)
                        T.op("pool", lambda e, up=up, ch=ch: e.tensor_copy(out=up.ap[:, 0:2], in_=halo.ap[:, ch, :]),
                             reads=hp, writes=up.pg(0, 2))
                        for tt in range(2):
                            T.op("act", lambda e, up=up, tt=tt, gv=gv, banks=banks: e.copy(
                                out=up.ap[:, 2 + tt * 512: 2 + (tt + 1) * 512], in_=ps[banks[tt * 2 + gv]][:, :]),
                                reads=[PS(banks[tt * 2 + gv])], writes=up.pg(2 + tt * 512, 2 + (tt + 1) * 512))
                        T.op("pool", lambda e, up=up, ch=ch: e.tensor_copy(out=halo.ap[:, ch, :], in_=up.ap[:, 1024:1026]),
                             reads=up.pg(1024, 1026), writes=hp)
                        T.op("pool", lambda e, up=up, acc=acc, ch=ch: e.tensor_scalar(
                            out=acc.ap, in0=up.ap[:, 0:1024], scalar1=pvc(fc + 0 * 44 + ch), scalar2=pvc(fc + 132 + ch),
                            op0=ALU.mult, op1=ALU.add), reads=up.pg() + pv.pg(), writes=acc.pg())
                        T.op("dve", lambda e, up=up, acc=acc, ch=ch: e.scalar_tensor_tensor(
                            out=acc.ap, in0=up.ap[:, 1:1025], scalar=pvc(fc + 1 * 44 + ch), in1=acc.ap, op0=ALU.mult, op1=ALU.add),
                            reads=up.pg() + acc.pg() + pv.pg(), writes=acc.pg())
                        T.op("dve", lambda e, up=up, acc=acc, ch=ch: e.scalar_tensor_tensor(
                            out=acc.ap, in0=up.ap[:, 2:1026], scalar=pvc(fc + 2 * 44 + ch), in1=acc.ap, op0=ALU.mult, op1=ALU.add),
                            reads=up.pg() + acc.pg() + pv.pg(), writes=acc.pg())
                        accs.append(acc)
                    sg = sgb[m % 2]
                    T.op("act", lambda e, sg=sg, a0=accs[0]: e.activation(out=sg.ap, in_=a0.ap, func=AF.Silu),
                         reads=accs[0].pg(), writes=sg.pg())
                    T.op("dve", lambda e, sg=sg, a1=accs[1], m=m: e.tensor_tensor(out=act.ap[:, m, :], in0=sg.ap, in1=a1.ap, op=ALU.mult),
                         reads=sg.pg() + accs[1].pg(), writes=act.pg(m * 1024, (m + 1) * 1024))
                na = 0
                for c in range(KC):
                    wd = wdb[c % 2]
                    T.dma("sp", wd.ap.rearrange("p a b -> p (a b)"), s_wdn[l, c], reads=[("W", "dn", l, c)], writes=wd.pg())
                    for tt in range(2):
                        bank = na % 8
                        na += 1
                        for m in range(MC):
                            T.op("pe", lambda e, wd=wd, m=m, tt=tt, bank=bank: e.matmul(
                                ps[bank][:, :], lhsT=wd.ap[:, m, :], rhs=act.ap[:, m, tt * 512:(tt + 1) * 512],
                                start=(m == 0), stop=(m == MC - 1)),
                                reads=wd.pg(m * 128, (m + 1) * 128) + act.pg(m * 1024 + tt * 512, m * 1024 + (tt + 1) * 512),
                                writes=[PS(bank)])
                        a0 = T0 + tt * 512
                        T.op("dve", lambda e, c=c, bank=bank, a0=a0: e.tensor_tensor(
                            out=xs.ap[:, c, a0:a0 + 512], in0=ps[bank][:, :], in1=xs.ap[:, c, a0:a0 + 512], op=ALU.add),
                            reads=[PS(bank)] + xs_pg(c, a0, a0 + 512), writes=xs_pg(c, a0, a0 + 512))
            A.top = mark

        def phase_attn():
            mark = A.top
            xn = A.alloc([KC, S], BF16)
            kTb = [A.alloc([S], BF16) for _ in range(2)]
            qTb = [A.alloc([2, S], BF16) for _ in range(2)]
            qTf = A.alloc([1024], F32)
            Vab = [A.alloc([16, 2, 128], BF16) for _ in range(2)]
            wst = [A.alloc([KC, 128], F32) for _ in range(2)]
            wqkv = [A.alloc([KC, 128], BF16) for _ in range(3)]
            negT = A.alloc([1024], BF16)
            Pb = [A.alloc([256], BF16) for _ in range(4)]
            rcb = [A.alloc([256], F32) for _ in range(2)]
            aob = [A.alloc([S], BF16) for _ in range(2)]
            wosl = [A.alloc([D], BF16) for _ in range(2)]
            Tzb = [A.alloc([2, 512], BF16) for _ in range(2)]
            g8 = A.alloc([16, 8], F32)
            mx = A.alloc([16, 8], F32)
            selb = A.alloc([16, 8], F32)
            negm = A.alloc([128], F32)

            for vb in Vab:
                T.op("pool", lambda e, vb=vb: e.memset(vb.ap[:, :, :, 64:128], 1.0), writes=vb.pg())
            for qb in qTb:
                T.op("pool", lambda e, qb=qb: e.memset(qb.ap, 0.0), writes=qb.pg())
            T.op("pool", lambda e: e.memset(negT.ap, 0.0), writes=negT.pg())
            T.op("pool", lambda e: e.memset(ksum.ap, 0.0), writes=ksum.pg())
            for t in range(4):
                rs = rms_stats(t * 512, 512, "pool", nk())
                for c in range(KC):
                    T.op("dve", lambda e, c=c, t=t, rs=rs: e.tensor_tensor(
                        out=xn.ap[:, c, t * 512:(t + 1) * 512], in0=xs.ap[:, c, t * 512:(t + 1) * 512], in1=rs.ap, op=ALU.mult),
                        reads=xs_pg(c, t * 512, (t + 1) * 512) + rs.pg(), writes=xn.pg(c * S + t * 512, c * S + (t + 1) * 512))
            pj = [0]
            cnt = [0, 0]
            nst = [0]
            for p in range(attn_pairs):
                kT = kTb[p % 2]
                qT = qTb[p % 2]
                Va = Vab[p % 2]
                ao = aob[p % 2]
                Tz = Tzb[p % 2]
                if dbg < 1:
                    continue
                for wi, gcol in enumerate((G_B, G_KV, G_KV)):
                    stg = wst[nst[0] % 2]
                    nst[0] += 1
                    T.dma("sp", stg.ap.rearrange("p a b -> p (a b)"), wqkv_d[wi][p], writes=stg.pg())
                    for kc in range(KC):
                        T.op("pool", lambda e, wi=wi, kc=kc, stg=stg, gcol=gcol: e.tensor_scalar(
                            out=wqkv[wi].ap[:, kc, :], in0=stg.ap[:, kc, :], scalar1=pvc(gcol + kc), scalar2=None, op0=ALU.mult),
                            reads=stg.pg(kc * 128, (kc + 1) * 128) + pv.pg(), writes=wqkv[wi].pg(kc * 128, (kc + 1) * 128))
                for h in range(2):
                    hg = 2 * p + h
                    T.dma("sp", Tz.ap[:, h, :], bass.AP(gs_t, hg * 131072 + 127, [[1023, 128], [1, 512]]),
                          reads=[("W", "gs")], writes=Tz.pg(h * 512, (h + 1) * 512))
                T.dma("sp", wosl[p % 2].ap, s_wo[p * 128:(p + 1) * 128, :], reads=[("W", "wo")], writes=wosl[p % 2].pg())
                for t in range(4):
                    bank = pj[0] % 2
                    pj[0] += 1
                    for kc in range(KC):
                        T.op("pe", lambda e, kc=kc, t=t, bank=bank: e.matmul(
                            ps[bank][:, :], lhsT=wqkv[1].ap[:, kc, :], rhs=xn.ap[:, kc, t * 512:(t + 1) * 512],
                            start=(kc == 0), stop=(kc == KC - 1)),
                            reads=wqkv[1].pg(kc * 128, (kc + 1) * 128) + xn.pg(kc * S + t * 512, kc * S + (t + 1) * 512),
                            writes=[PS(bank)])
                    T.op("act", lambda e, t=t, bank=bank, kT=kT: e.copy(out=kT.ap[:, t * 512:(t + 1) * 512], in_=ps[bank][:, :]),
                         reads=[PS(bank)], writes=kT.pg(t * 512, (t + 1) * 512))
                    for h in range(2):
                        T.op("dve", lambda e, t=t, bank=bank, h=h: e.tensor_reduce(
                            out=ksum.ap[h * 64:(h + 1) * 64, h, 2 * t:2 * t + 2],
                            in_=ps[bank][h * 64:(h + 1) * 64, :].rearrange("p (a b) -> p a b", a=2), axis=AX.X, op=ALU.add),
                            reads=[PS(bank)], writes=ksum.pg())
                for t in range(4):
                    bank = pj[0] % 2
                    pj[0] += 1
                    for kc in range(KC):
                        T.op("pe", lambda e, kc=kc, t=t, bank=bank: e.matmul(
                            ps[bank][:, :], lhsT=wqkv[0].ap[:, kc, :], rhs=xn.ap[:, kc, t * 512:(t + 1) * 512],
                            start=(kc == 0), stop=(kc == KC - 1)),
                            reads=wqkv[0].pg(kc * 128, (kc + 1) * 128) + xn.pg(kc * S + t * 512, kc * S + (t + 1) * 512),
                            writes=[PS(bank)])
                    for h in range(2):
                        T.op("act", lambda e, t=t, bank=bank, qT=qT, h=h: e.mul(
                            out=qT.ap[h * 64:(h + 1) * 64, h, t * 512:(t + 1) * 512], in_=ps[bank][h * 64:(h + 1) * 64, :], mul=0.125),
                            reads=[PS(bank)], writes=qT.pg(h * S + t * 512, h * S + (t + 1) * 512))
                    if t >= 2:
                        T.op("dve", lambda e, t=t, bank=bank: e.tensor_scalar(
                            out=qTf.ap[:, (t - 2) * 512:(t - 1) * 512], in0=ps[bank][:, :], scalar1=0.125, scalar2=None, op0=ALU.mult),
                            reads=[PS(bank)], writes=qTf.pg((t - 2) * 512, (t - 1) * 512))
                for k4 in range(4):
                    bank = pj[0] % 2
                    pj[0] += 1
                    for i in range(4):
                        kt = k4 * 4 + i
                        for kc in range(KC):
                            T.op("pe", lambda e, kc=kc, kt=kt, i=i, bank=bank: e.matmul(
                                ps[bank][:, i * 128:(i + 1) * 128], lhsT=xn.ap[:, kc, kt * 128:(kt + 1) * 128], rhs=wqkv[2].ap[:, kc, :],
                                start=(kc == 0), stop=(kc == KC - 1)),
                                reads=wqkv[2].pg(kc * 128, (kc + 1) * 128) + xn.pg(kc * S + kt * 128, kc * S + (kt + 1) * 128),
                                writes=[PS(bank)])
                    T.op("act", lambda e, k4=k4, bank=bank, Va=Va: e.copy(
                        out=Va.ap[:, k4 * 4:(k4 + 1) * 4, :, 0:64], in_=ps[bank][:, :].rearrange("p (i h d) -> p i h d", i=4, h=2)),
                        reads=[PS(bank)], writes=Va.pg(k4 * 4 * 256, (k4 + 1) * 4 * 256))
                if dbg < 2:
                    continue
                for ch in range(8):
                    for h in range(2):
                        T.op("pe", lambda e, ch=ch, h=h: e.matmul(
                            ps[2][:, ch * 16 + h * 8: ch * 16 + h * 8 + 8], lhsT=qTf.ap[:, ch * 128:(ch + 1) * 128],
                            rhs=ksum.ap[:, h, :], start=True, stop=True),
                            reads=qTf.pg(ch * 128, (ch + 1) * 128) + ksum.pg(), writes=[PS(2)])
                T.op("dve", lambda e: e.tensor_copy(out=g8.ap.rearrange("p a b -> p (a b)"), in_=ps[2][:, 0:128]),
                     reads=[PS(2)], writes=g8.pg())
                g8v = g8.ap.rearrange("p (c h) n -> p c h n", h=2)
                for own in range(4, 8):
                    c0 = (own - 4) * 2
                    T.op("dve", lambda e, own=own, c0=c0: e.memset(g8v[:, c0:c0 + 2, :, own:8], -1e30), writes=g8.pg())
                for i in range(16):
                    T.op("dve", lambda e, i=i: e.max(out=mx.ap[:, i, :], in_=g8.ap[:, i, :]), reads=g8.pg(), writes=mx.pg())
                T.op("dve", lambda e: e.tensor_tensor(out=selb.ap, in0=g8.ap, in1=mx.ap[:, :, 2:3].to_broadcast([128, 16, 8]), op=ALU.is_ge),
                     reads=g8.pg() + mx.pg(), writes=selb.pg())
                T.op("dve", lambda e: e.tensor_scalar(out=negm.ap, in0=selb.ap.rearrange("p a b -> p (a b)"),
                                                      scalar1=-NEGM, scalar2=NEGM, op0=ALU.mult, op1=ALU.add),
                     reads=selb.pg(), writes=negm.pg())
                for hf in range(2):
                    for i in range(4):
                        ch = hf * 4 + i
                        T.op("pe", lambda e, ch=ch, i=i: e.transpose(ps[2][0:16, i * 128:(i + 1) * 128], negm.ap[:, ch * 16:(ch + 1) * 16], idf.ap),
                             reads=negm.pg() + idf.pg(), writes=[PS(2)])
                    T.op("dve", lambda e, hf=hf: e.tensor_copy(out=negT.ap[0:16, hf * 512:(hf + 1) * 512], in_=ps[2][0:16, 0:512]),
                         reads=[PS(2)], writes=negT.pg(hf * 512, (hf + 1) * 512))
                if dbg < 3:
                    continue
                for h in range(2):
                    hs = slice(h * 64, (h + 1) * 64)
                    for o in range(8):
                        q0 = o * 256
                        pob = 3 + (cnt[0] % 2)
                        cnt[0] += 1
                        nkt = 2 * (o + 1)
                        for kt in range(nkt):
                            sbk = 5 + (cnt[1] % 3)
                            pt = Pb[cnt[1] % 4]
                            cnt[1] += 1
                            mm = [(kT.ap[:, kt * 128:(kt + 1) * 128], qT.ap[:, h, q0:q0 + 256],
                                   kT.pg(kt * 128, (kt + 1) * 128) + qT.pg(h * S + q0, h * S + q0 + 256))]
                            if kt >= 2 * o - 1:
                                off = (q0 - kt * 128) + 128
                                mm.append((idb.ap, Tz.ap[:, h, off:off + 256], idb.pg() + Tz.pg(h * 512 + off, h * 512 + off + 256)))
                            if o >= 4 and kt < 2 * o:
                                mm.append((Z.ap[:, h * 8 + kt // 2, :], negT.ap[:, q0 - 1024:q0 - 1024 + 256],
                                           Z.pg() + negT.pg(q0 - 1024, q0 - 1024 + 256)))
                            for i, (l_, r_, rd) in enumerate(mm):
                                T.op("pe", lambda e, l_=l_, r_=r_, i=i, n=len(mm), sbk=sbk: e.matmul(
                                    ps[sbk][:, 0:256], lhsT=l_, rhs=r_, start=(i == 0), stop=(i == n - 1)),
                                    reads=rd, writes=[PS(sbk)])
                            T.op("act", lambda e, pt=pt, sbk=sbk: e.activation(out=pt.ap, in_=ps[sbk][:, 0:256], func=AF.Exp),
                                 reads=[PS(sbk)], writes=pt.pg())
                            T.op("pe", lambda e, pt=pt, kt=kt, h=h, pob=pob, nkt=nkt, Va=Va: e.matmul(
                                ps[pob][:, 0:256], lhsT=Va.ap[:, kt, h, :], rhs=pt.ap, start=(kt == 0), stop=(kt == nkt - 1)),
                                reads=pt.pg() + Va.pg(kt * 256 + h * 128, kt * 256 + (h + 1) * 128), writes=[PS(pob)])
                        rc = rcb[cnt[0] % 2]
                        T.op("dve", lambda e, rc=rc, pob=pob: e.reciprocal(out=rc.ap[64:128, :], in_=ps[pob][64:128, 0:256]),
                             reads=[PS(pob)], writes=rc.pg())
                        T.op("dve", lambda e, rc=rc, pob=pob, hs=hs, q0=q0, ao=ao: e.tensor_tensor(
                            out=ao.ap[hs, q0:q0 + 256], in0=ps[pob][0:64, 0:256], in1=rc.ap[64:128, :], op=ALU.mult),
                            reads=[PS(pob)] + rc.pg(), writes=ao.pg(q0, q0 + 256))
                if dbg < 4:
                    continue
                for t in range(4):
                    for c in range(KC):
                        bank = pj[0] % 2
                        pj[0] += 1
                        T.op("pe", lambda e, c=c, t=t, bank=bank, ao=ao, w=wosl[p % 2]: e.matmul(
                            ps[bank][:, :], lhsT=w.ap[:, c * 128:(c + 1) * 128], rhs=ao.ap[:, t * 512:(t + 1) * 512], start=True, stop=True),
                            reads=wosl[p % 2].pg(c * 128, (c + 1) * 128) + ao.pg(t * 512, (t + 1) * 512), writes=[PS(bank)])
                        T.op("dve", lambda e, c=c, t=t, bank=bank: e.tensor_tensor(
                            out=xs.ap[:, c, t * 512:(t + 1) * 512], in0=ps[bank][:, :], in1=xs.ap[:, c, t * 512:(t + 1) * 512], op=ALU.add),
                            reads=[PS(bank)] + xs_pg(c, t * 512, (t + 1) * 512), writes=xs_pg(c, t * 512, (t + 1) * 512))
            A.top = mark

        def phase_out(s, do_norm):
            mark = A.top
            ob = [A.alloc([512], F32) for _ in range(3)]
            k = 0
            outs = []
            for t in range(S // 512):
                t0 = t * 512
                if do_norm:
                    rs = rms_stats(t0, 512, "pool", nk())
                for c in range(KC):
                    o_ = ob[k % 3]
                    k += 1
                    if do_norm:
                        T.op("dve", lambda e, c=c, t0=t0, o_=o_, rs=rs: e.scalar_tensor_tensor(
                            out=o_.ap, in0=xs.ap[:, c, t0:t0 + 512], scalar=pvc(G_FIN + c), in1=rs.ap, op0=ALU.mult, op1=ALU.mult),
                            reads=xs_pg(c, t0, t0 + 512) + rs.pg() + pv.pg(), writes=o_.pg())
                    else:
                        T.op("dve", lambda e, c=c, t0=t0, o_=o_: e.tensor_copy(out=o_.ap, in_=xs.ap[:, c, t0:t0 + 512]),
                             reads=xs_pg(c, t0, t0 + 512), writes=o_.pg())
                    outs.append(T.dma("sp", yT_d[s, c * 128:(c + 1) * 128, t0:t0 + 512], o_.ap, reads=o_.pg()))
            A.top = mark
            return outs

        final = []
        for s in range(nseq):
            for c in range(KC):
                T.dma("sp", xs.ap[:, c, :], xT_d[s, c * 128:(c + 1) * 128, :], writes=xs_pg(c, 0, S))
            if "mix" in phases:
                phase_mixer()
            if "ffn0" in phases:
                phase_ffn(0)
            if not late_cast_done[0]:
                if "ffn1" in phases:
                    cast_layer(1)
                late_cast_done[0] = True
            if "attn" in phases:
                phase_attn()
            if "ffn1" in phases:
                phase_ffn(1)
            final += phase_out(s, final_norm)
            if s < nseq - 1:
                T.mark()
        counts = T.finalize_and_emit(final_waits=final)
    return nc, counts


def _rel_bucket(n):
    import math
    n = max(n, 0)
    if n < 16:
        return n
    v = 16 + int(np.float32(np.log(np.float32(n) / np.float32(16)) / np.float32(math.log(128 / 16))) * np.float32(16))
    return min(v, 31)


def _const_tables():
    oht = np.zeros((33, 1024), np.float32)
    for m in range(1024):
        d = m - 255
        if d < 0:
            oht[32, m] = NEGM
        else:
            b = _rel_bucket(d)
            oht[b, m] += 1.0
            oht[31, m] -= 1.0
    return oht, np.eye(128, dtype=np.float32)


def prep_shared(a_norm, a_w_in, a_conv, a_w_out, kv_norm, w_k, w_v, b_norm, b_w_q, b_w_o, rel_bias,
                f_norm, f_w_up, f_conv, f_conv_b, f_w_down, final_norm):
    f = np.float32
    pvm = np.zeros((128, NPV), f)

    def put(col, vec):
        n = vec.shape[0] // 128
        pvm[:, col:col + n] = np.asarray(vec, f).reshape(n, 128).T

    put(G_A, a_norm[0]); put(G_F0, f_norm[0]); put(G_F1, f_norm[1]); put(G_KV, kv_norm)
    put(G_B, b_norm[0]); put(G_FIN, final_norm)
    for j in range(3):
        put(AC + j * 8, a_conv[0, j])
    for l in range(2):
        base = FC_BASE + l * 176
        for j in range(3):
            put(base + j * 44, f_conv[l, j])
        put(base + 132, f_conv_b[l])
    wup_s = np.empty((2, MC, 128, KC * 256), f)
    wdn_s = np.empty((2, KC, 128, MC * 128), f)
    for l in range(2):
        w = np.asarray(f_w_up[l], f).reshape(KC, 128, 2, MC, 128)
        wup_s[l] = w.transpose(3, 1, 0, 2, 4).reshape(MC, 128, KC * 256)
        w = np.asarray(f_w_down[l], f).reshape(MC, 128, KC, 128)
        wdn_s[l] = w.transpose(2, 1, 0, 3).reshape(KC, 128, MC * 128)

    def pair_slabs(w):
        w = np.asarray(w, f).reshape(KC, 128, 8, 128)
        return np.ascontiguousarray(w.transpose(2, 1, 0, 3).reshape(8, 128, KC * 128))

    oht, ident = _const_tables()
    rb_aug = np.concatenate([np.asarray(rel_bias, f), np.ones((1, 16), f)], axis=0)
    return {
        "win": np.ascontiguousarray(a_w_in[0], f), "wout": np.ascontiguousarray(a_w_out[0], f),
        "wo": np.ascontiguousarray(b_w_o[0], f), "wup_s": wup_s, "wdn_s": wdn_s,
        "wq_s": pair_slabs(b_w_q[0]), "wk_s": pair_slabs(w_k), "wv_s": pair_slabs(w_v),
        "pvec": pvm, "rb_aug": rb_aug, "oht": oht, "ident": ident,
    }


_PROG = {}


def kernel(x, a_norm, a_w_in, a_conv, a_w_out, kv_norm, w_k, w_v, b_norm, b_w_q, b_w_o, rel_bias,
           f_norm, f_w_up, f_conv, f_conv_b, f_w_down, final_norm):
    x = np.asarray(x, np.float32)
    shared = prep_shared(a_norm, a_w_in, a_conv, a_w_out, kv_norm, w_k, w_v, b_norm, b_w_q, b_w_o, rel_bias,
                         f_norm, f_w_up, f_conv, f_conv_b, f_w_down, final_norm)
    if "nc" not in _PROG:
        _PROG["nc"] = build_program(nseq=2)[0]
    nc = _PROG["nc"]
    in_maps = []
    for c in range(NCORES):
        xc = np.ascontiguousarray(x[2 * c:2 * c + 2].transpose(0, 2, 1))
        m = dict(shared)
        m["xT"] = xc
        in_maps.append(m)
    res = run_bass_kernel_spmd(nc, in_maps, core_ids=list(range(NCORES)))
    out = np.empty((16, S, D), np.float32)
    for c in range(NCORES):
        y = res.results[c]["yT"]
        out[2 * c:2 * c + 2] = np.asarray(y).transpose(0, 2, 1)
    return out
```
